# Optimizing a Trainium2 kernel written in Bass

```python
import jax, jax.numpy as jnp
from jax import lax
import numpy as np

D_MODEL = 1024
BATCH = 16
SEQ = 2048
DEPTH = 4

CHUNK = 64
Q_BLOCK = 128
N_A_LAYERS = DEPTH // 2
N_B_LAYERS = DEPTH - N_A_LAYERS
D_FF = 2816
NORM_EPS = 1e-6
GDN_HEADS = 8
GDN_DK = 128
GDN_DV = 128
GDN_CONV = 4
GDN_QK = GDN_HEADS * GDN_DK
GDN_V = GDN_HEADS * GDN_DV
GDN_QKV = 2 * GDN_QK + GDN_V
GDN_IN = GDN_QKV + GDN_V + 2 * GDN_HEADS
MLA_HEADS = 8
MLA_NOPE = 128
MLA_ROPE = 64
MLA_V = 128
MLA_Q_RANK = 384
MLA_KV_RANK = 256
ROPE_THETA = 10000.0
MAX_POS_OFFSET = 65536

kernel_name = "yoco_gated_deltanet_mla_macaron"


def rms_norm(x, g):
    xf = x.astype(jnp.float32)
    y = xf * lax.rsqrt(jnp.mean(xf * xf, axis=-1, keepdims=True) + NORM_EPS)
    return (y * g.astype(jnp.float32)).astype(x.dtype)


def l2_norm(x):
    xf = x.astype(jnp.float32)
    return xf * lax.rsqrt(jnp.sum(xf * xf, axis=-1, keepdims=True) + NORM_EPS)


def swiglu(x, w_gu, w_down):
    g, u = jnp.split(x @ w_gu, 2, axis=-1)
    return (jax.nn.silu(g) * u) @ w_down


def causal_dwconv(x, w):
    K, C = w.shape
    return lax.conv_general_dilated(x, w[:, None, :].astype(x.dtype), window_strides=(1,),
                                    padding=[(K - 1, 0)],
                                    dimension_numbers=('NWC', 'WIO', 'NWC'),
                                    feature_group_count=C)


def rope(x, pos):
    half = x.shape[-1] // 2
    inv = ROPE_THETA ** (-jnp.arange(half, dtype=jnp.float32) / half)
    ang = pos.astype(jnp.float32)[..., None] * inv
    ang = ang.reshape(ang.shape[:2] + (1,) * (x.ndim - 3) + (half,))
    cos, sin = jnp.cos(ang), jnp.sin(ang)
    xf = x.astype(jnp.float32)
    x1, x2 = xf[..., :half], xf[..., half:]
    return jnp.concatenate([x1 * cos - x2 * sin, x2 * cos + x1 * sin], axis=-1).astype(x.dtype)


def gated_delta_rule(q, k, v, g, beta):
    f32 = jnp.float32
    Bn, S, H, DK = q.shape
    DV = v.shape[-1]
    N = S // CHUNK

    def chunks(t):
        t = t.astype(f32).reshape((Bn, N, CHUNK, H) + t.shape[3:])
        return jnp.moveaxis(t, (1, 3), (0, 2))

    qc, kc, vc = chunks(q), chunks(k), chunks(v)
    gc = jnp.cumsum(chunks(g), axis=-1)
    bc = chunks(beta)
    idx = jnp.arange(CHUNK)
    causal = idx[:, None] >= idx[None, :]
    strict = idx[:, None] > idx[None, :]
    gdiff = gc[..., :, None] - gc[..., None, :]
    decay = jnp.where(causal, jnp.exp(jnp.where(causal, gdiff, 0.0)), 0.0)
    k_beta = kc * bc[..., None]
    a_mat = jnp.where(strict, jnp.einsum('nbhid,nbhjd->nbhij', k_beta, kc) * decay, 0.0)
    eye = jnp.eye(CHUNK, dtype=f32)
    rhs = jnp.concatenate([vc * bc[..., None], k_beta * jnp.exp(gc)[..., None]], axis=-1)
    sol = lax.linalg.triangular_solve(a_mat + eye, rhs, left_side=True, lower=True)
    u, w = sol[..., :DV], sol[..., DV:]
    attn_intra = jnp.einsum('nbhid,nbhjd->nbhij', qc, kc) * decay
    q_dec = qc * jnp.exp(gc)[..., None]
    g_last = gc[..., -1]
    k_dec = kc * jnp.exp(g_last[..., None] - gc)[..., None]

    def step(state, inp):
        u_i, w_i, qd_i, a_i, kd_i, gl_i = inp
        v_new = u_i - jnp.einsum('bhcd,bhde->bhce', w_i, state)
        o = jnp.einsum('bhcd,bhde->bhce', qd_i, state) + jnp.einsum('bhij,bhje->bhie', a_i, v_new)
        state = state * jnp.exp(gl_i)[..., None, None] + jnp.einsum('bhcd,bhce->bhde', kd_i, v_new)
        return state, o

    s0 = jnp.zeros((Bn, H, DK, DV), f32)
    _, o = lax.scan(step, s0, (u, w, q_dec, attn_intra, k_dec, g_last))
    return jnp.moveaxis(o, (0, 2), (1, 3)).reshape(Bn, S, H, DV)


def gdn_mixer(h, w_in, conv_w, a_log, dt_bias, out_norm, w_out):
    Bn, S, _ = h.shape
    proj = h @ w_in
    qkv, gate, a, b = jnp.split(proj, [GDN_QKV, GDN_QKV + GDN_V, GDN_QKV + GDN_V + GDN_HEADS], axis=-1)
    qkv = jax.nn.silu(causal_dwconv(qkv, conv_w))
    q, k, v = jnp.split(qkv, [GDN_QK, 2 * GDN_QK], axis=-1)
    q = l2_norm(q.reshape(Bn, S, GDN_HEADS, GDN_DK)) * (GDN_DK ** -0.5)
    k = l2_norm(k.reshape(Bn, S, GDN_HEADS, GDN_DK))
    v = v.reshape(Bn, S, GDN_HEADS, GDN_DV)
    g = -jnp.exp(a_log.astype(jnp.float32)) * jax.nn.softplus(a.astype(jnp.float32) + dt_bias.astype(jnp.float32))
    beta = jax.nn.sigmoid(b.astype(jnp.float32))
    o = gated_delta_rule(q, k, v, g, beta)
    o = rms_norm(o, out_norm) * jax.nn.silu(gate.reshape(Bn, S, GDN_HEADS, GDN_DV).astype(jnp.float32))
    return o.reshape(Bn, S, GDN_V).astype(h.dtype) @ w_out


def mla_shared_kv(h, positions, kv_norm, w_kv_a, kv_a_norm, w_kv_b):
    Bn, S, _ = h.shape
    kv_a = rms_norm(h, kv_norm) @ w_kv_a
    c_kv, k_rope = kv_a[..., :MLA_KV_RANK], kv_a[..., MLA_KV_RANK:]
    c_kv = rms_norm(c_kv, kv_a_norm)
    k_rope = rope(k_rope, positions)
    kv = (c_kv @ w_kv_b).reshape(Bn, S, MLA_HEADS, MLA_NOPE + MLA_V)
    return kv[..., :MLA_NOPE], k_rope, kv[..., MLA_NOPE:]


def mla_mixer(h, positions, w_dq, q_norm, w_uq, w_o, k_nope, k_rope, v):
    Bn, S, _ = h.shape
    q = (rms_norm(h @ w_dq, q_norm) @ w_uq).reshape(Bn, S, MLA_HEADS, MLA_NOPE + MLA_ROPE)
    q_nope = q[..., :MLA_NOPE]
    q_rope = rope(q[..., MLA_NOPE:], positions)
    scale = (MLA_NOPE + MLA_ROPE) ** -0.5
    outs = []
    for blk in range(S // Q_BLOCK):
        q0, q1 = blk * Q_BLOCK, (blk + 1) * Q_BLOCK
        s = (jnp.einsum('bqhd,bkhd->bhqk', q_nope[:, q0:q1], k_nope[:, :q1])
             + jnp.einsum('bqhr,bkr->bhqk', q_rope[:, q0:q1], k_rope[:, :q1])).astype(jnp.float32) * scale
        mask = (jnp.arange(q0, q1) // CHUNK)[:, None] >= (jnp.arange(q1) // CHUNK)[None, :]
        p = jax.nn.softmax(jnp.where(mask, s, -jnp.inf), axis=-1).astype(v.dtype)
        outs.append(jnp.einsum('bhqk,bkhe->bqhe', p, v[:, :q1]))
    o = jnp.concatenate(outs, axis=1).reshape(Bn, S, MLA_HEADS * MLA_V)
    return o @ w_o


def setup_inputs(seed: int = 0) -> dict:
    key = jax.random.key(seed)
    ks = jax.random.split(key, 32)
    f32 = jnp.float32

    def nrm(k, shape, fan_in):
        return jax.random.normal(k, shape, f32) * (fan_in ** -0.5)

    def gain(k, shape):
        return 1.0 + 0.02 * jax.random.normal(k, shape, f32)

    x = jax.random.normal(ks[0], (BATCH, SEQ, D_MODEL), f32)
    offset = jax.random.randint(ks[1], (BATCH,), 0, MAX_POS_OFFSET // CHUNK, dtype=jnp.int32) * CHUNK
    positions = (offset[:, None] + jnp.arange(SEQ, dtype=jnp.int32)[None, :]).astype(jnp.int32)
    dt = jnp.exp(jax.random.uniform(ks[2], (N_A_LAYERS, GDN_HEADS), f32, jnp.log(1e-3), jnp.log(1e-1)))
    return {
        "x": x,
        "positions": positions,
        "ffn1_norm": gain(ks[3], (DEPTH, D_MODEL)),
        "ffn1_w_gu": nrm(ks[4], (DEPTH, D_MODEL, 2 * D_FF), D_MODEL),
        "ffn1_w_down": nrm(ks[5], (DEPTH, D_FF, D_MODEL), D_FF),
        "mix_norm": gain(ks[6], (DEPTH, D_MODEL)),
        "ffn2_norm": gain(ks[7], (DEPTH, D_MODEL)),
        "ffn2_w_gu": nrm(ks[8], (DEPTH, D_MODEL, 2 * D_FF), D_MODEL),
        "ffn2_w_down": nrm(ks[9], (DEPTH, D_FF, D_MODEL), D_FF),
        "gdn_w_in": nrm(ks[10], (N_A_LAYERS, D_MODEL, GDN_IN), D_MODEL),
        "gdn_conv_w": nrm(ks[11], (N_A_LAYERS, GDN_CONV, GDN_QKV), GDN_CONV),
        "gdn_a_log": jnp.log(jax.random.uniform(ks[12], (N_A_LAYERS, GDN_HEADS), f32, 1.0, 16.0)),
        "gdn_dt_bias": dt + jnp.log(-jnp.expm1(-dt)),
        "gdn_out_norm": gain(ks[13], (N_A_LAYERS, GDN_DV)),
        "gdn_w_out": nrm(ks[14], (N_A_LAYERS, GDN_V, D_MODEL), GDN_V),
        "kv_norm": gain(ks[15], (D_MODEL,)),
        "mla_w_kv_a": nrm(ks[16], (D_MODEL, MLA_KV_RANK + MLA_ROPE), D_MODEL),
        "mla_kv_a_norm": gain(ks[17], (MLA_KV_RANK,)),
        "mla_w_kv_b": nrm(ks[18], (MLA_KV_RANK, MLA_HEADS * (MLA_NOPE + MLA_V)), MLA_KV_RANK),
        "mla_w_dq": nrm(ks[19], (N_B_LAYERS, D_MODEL, MLA_Q_RANK), D_MODEL),
        "mla_q_norm": gain(ks[20], (N_B_LAYERS, MLA_Q_RANK)),
        "mla_w_uq": nrm(ks[21], (N_B_LAYERS, MLA_Q_RANK, MLA_HEADS * (MLA_NOPE + MLA_ROPE)), MLA_Q_RANK),
        "mla_w_o": nrm(ks[22], (N_B_LAYERS, MLA_HEADS * MLA_V, D_MODEL), MLA_HEADS * MLA_V),
        "final_norm": gain(ks[23], (D_MODEL,)),
    }


def reference(x, positions, ffn1_norm, ffn1_w_gu, ffn1_w_down, mix_norm, ffn2_norm, ffn2_w_gu, ffn2_w_down,
              gdn_w_in, gdn_conv_w, gdn_a_log, gdn_dt_bias, gdn_out_norm, gdn_w_out,
              kv_norm, mla_w_kv_a, mla_kv_a_norm, mla_w_kv_b,
              mla_w_dq, mla_q_norm, mla_w_uq, mla_w_o, final_norm):
    h = x
    shared = None
    for layer in range(DEPTH):
        h = h + 0.5 * swiglu(rms_norm(h, ffn1_norm[layer]), ffn1_w_gu[layer], ffn1_w_down[layer])
        hn = rms_norm(h, mix_norm[layer])
        if layer < N_A_LAYERS:
            i = layer
            h = h + gdn_mixer(hn, gdn_w_in[i], gdn_conv_w[i], gdn_a_log[i], gdn_dt_bias[i],
                              gdn_out_norm[i], gdn_w_out[i])
        else:
            j = layer - N_A_LAYERS
            k_nope, k_rope, v = shared
            h = h + mla_mixer(hn, positions, mla_w_dq[j], mla_q_norm[j], mla_w_uq[j], mla_w_o[j],
                              k_nope, k_rope, v)
        h = h + 0.5 * swiglu(rms_norm(h, ffn2_norm[layer]), ffn2_w_gu[layer], ffn2_w_down[layer])
        if layer == N_A_LAYERS - 1:
            shared = mla_shared_kv(h, positions, kv_norm, mla_w_kv_a, mla_kv_a_norm, mla_w_kv_b)
    return rms_norm(h, final_norm)
```

```python
import numpy as np
import concourse.bass as bass
import concourse.mybir as mybir
from concourse.bass_utils import run_bass_kernel_spmd

F32 = mybir.dt.float32
BF16 = mybir.dt.bfloat16
I32 = mybir.dt.int32
AF = mybir.ActivationFunctionType
ALU = mybir.AluOpType

D = 1024
S = 2048
KC = 8
DFF = 2816
NJG = 11
NT512 = 4
EPS = 1e-6
BIG = 30000.0
NCORES = 8
TWO_PI = 2.0 * np.pi
C1_2PI = 6.28125
C2_2PI = float(TWO_PI - 6.28125)

CI = {}


def _build_consts():
    cols = []

    def add(name, arr):
        CI[name] = sum(a.shape[1] for a in cols)
        cols.append(arr.astype(np.float32))

    p = np.arange(128)[:, None]
    f = np.arange(128)[None, :]
    add("ident", (p == f))
    add("nident", -(p == f).astype(np.float32))
    add("ones", np.ones((128, 128)))
    add("tri", (p <= f))
    add("pos_ls", np.where(p > f, 0.0, BIG))
    add("neg_us", np.where(f > p, 0.0, -BIG))
    add("neg_ui", np.where(f >= p, 0.0, -BIG))
    bd = lambda b: ((p // b) == (f // b)).astype(np.float32)
    add("bd16", bd(16))
    add("m32", bd(32) - bd(16))
    add("m64", bd(64) - bd(32))
    add("m128", 1.0 - bd(64))
    half = 32
    inv = (10000.0 ** (-np.arange(half, dtype=np.float32) / half)).astype(np.float32)
    invp = np.zeros((128, 1), np.float32)
    invp[:64, 0] = np.concatenate([inv, inv])
    sgn = np.zeros((128, 1), np.float32)
    sgn[:32] = -1.0
    sgn[32:64] = 1.0
    add("inv", invp)
    add("sgn", sgn)
    add("eps", np.full((128, 1), EPS))
    add("one", np.full((128, 1), 1.0))
    return np.concatenate(cols, axis=1)


CONSTS = _build_consts()
NCONST = CONSTS.shape[1]


class V:
    __slots__ = ("ap", "keys", "meta")

    def __init__(self, ap, keys):
        self.ap = ap
        self.keys = tuple(keys)
        self.meta = None

    def __getitem__(self, idx):
        return V(self.ap[idx], self.keys)


class Op:
    __slots__ = ("fn", "deps", "inc", "semval", "waits", "real")

    def __init__(self, fn, deps, real=True):
        self.fn = fn
        self.deps = deps
        self.inc = False
        self.semval = 0
        self.waits = []
        self.real = real


ENGS = ("pe", "act", "dve", "pool", "sp")
SAME_ENGINE_SYNC = True
STRICT_SAME = True
NDSEM = 16
GDN_OFFSET = 13
MLA_OFFSET = 0


class Prog:
    def __init__(self, nc):
        self.nc = nc
        self.q = {e: [] for e in ENGS}
        self.wr = {}
        self.rd = {}
        self.dma_cnt = {}
        self.dma_rr = {}
        self.last_real = {}

    def reset_tracking(self):
        self.wr = {}
        self.rd = {}

    def _deps(self, reads, writes):
        deps = {}
        for r in reads:
            if r in self.wr:
                deps[self.wr[r]] = True
            if isinstance(r, tuple) and r[0] == "ps":
                for tok in self.rd.get(r, {}).values():
                    deps.setdefault(tok, False)
        for w in writes:
            if w in self.wr:
                deps[self.wr[w]] = deps.get(self.wr[w], False) or STRICT_SAME
            for tok in self.rd.get(w, {}).values():
                deps[tok] = deps.get(tok, False) or STRICT_SAME
        out = {}
        for (k, i), st in deps.items():
            t = (k, self.dma_cnt[k[4:]]) if k.startswith("dma:") else (k, i)
            out[t] = out.get(t, False) or st
        return out

    def _commit(self, tok, reads, writes, rkey):
        for w in writes:
            self.wr[w] = tok
            self.rd[w] = {}
        for r in reads:
            self.rd.setdefault(r, {})[rkey] = tok

    def op(self, eng, fn, reads=(), writes=()):
        reads = [k for v in reads for k in (v.keys if isinstance(v, V) else (v,))]
        writes = [k for v in writes for k in (v.keys if isinstance(v, V) else (v,))]
        deps = self._deps(reads, writes)
        o = Op(fn, deps)
        self.q[eng].append(o)
        tok = (eng, len(self.q[eng]) - 1)
        self.last_real[eng] = tok
        self._commit(tok, reads, writes, eng)
        return tok

    def dma(self, eng, sem, fns, reads=(), writes=()):
        reads = [k for v in reads for k in (v.keys if isinstance(v, V) else (v,))]
        writes = [k for v in writes for k in (v.keys if isinstance(v, V) else (v,))]
        deps = self._deps(reads, writes)
        rr = self.dma_rr.get(eng, 0)
        self.dma_rr[eng] = (rr + 1) % NDSEM
        sem = "%s_d%d" % (eng, rr)
        if sem in self.dma_cnt:
            deps[("dma:" + sem, self.dma_cnt[sem])] = True
        self.dma_cnt[sem] = self.dma_cnt.get(sem, 0) + 16 * len(fns)
        val = self.dma_cnt[sem]

        def fn(e, fns=fns, sem=sem):
            for f in fns:
                f(e).then_inc(self.sems[sem], 16)
            return None
        o = Op(fn, deps)
        self.q[eng].append(o)
        tok = ("dma:" + sem, val)
        self._commit(tok, reads, writes, "dma:" + sem)
        return tok

    def barrier(self):
        toks = dict(self.last_real)
        dtoks = [("dma:" + s, v) for s, v in self.dma_cnt.items()]
        for e in ENGS:
            deps = {t: True for k, t in toks.items() if k != e}
            deps.update({t: True for t in dtoks})
            self.q[e].append(Op(None, deps, real=False))
        self.reset_tracking()

    def wait_all(self, eng, toks):
        self.q[eng].append(Op(None, {t: True for t in toks}, real=False))

    def finalize(self, sems):
        self.sems = sems
        for e in ENGS:
            waited = {}
            for o in self.q[e]:
                best = {}
                for (k, idx), strong in o.deps.items():
                    if k == e and (e == "pe" or not strong or not SAME_ENGINE_SYNC):
                        continue
                    best[k] = max(best.get(k, -1), idx)
                for (k, idx) in sorted(best.items()):
                    if idx <= waited.get(k, -1 if not k.startswith("dma:") else 0):
                        continue
                    waited[k] = idx
                    if k.startswith("dma:"):
                        o.waits.append((k[4:], idx, None))
                    else:
                        tgt = self.q[k][idx]
                        assert tgt.real
                        tgt.inc = True
                        o.waits.append((k, None, tgt))
        for e in ENGS:
            c = 0
            for o in self.q[e]:
                if o.inc:
                    c += 1
                    o.semval = c

    def emit(self, eng_name, eng):
        for o in self.q[eng_name]:
            for (k, val, tgt) in o.waits:
                if tgt is None:
                    eng.wait_ge(self.sems[k], val)
                else:
                    eng.wait_ge(self.sems[k], tgt.semval)
            if o.fn is not None:
                ins = o.fn(eng)
                if o.inc:
                    ins.then_inc(self.sems[eng_name], 1)


class Cfg:
    def __init__(self, nseq=2, stop=None, dbg=False, phases=None):
        self.phases = phases
        self.nseq = nseq
        self.stop = stop
        self.dbg = dbg


def build(cfg):
    nc = bass.Bass("TRN2", target_bir_lowering=False)
    P = Prog(nc)
    NSEQ = cfg.nseq

    def din(name, shape, dt=F32):
        return nc.dram_tensor(name, list(shape), dt, kind="ExternalInput")

    x_fm = din("x_fm", [NSEQ, 128, KC, S])
    pos_d = din("pos", [NSEQ, S], I32)
    consts_d = din("consts", [128, NCONST])
    gains_d = din("gains", [128, 128])
    convw_d = din("convw", [128, 192])
    rows_d = din("rows", [2, 2, 128])
    wffn_d = din("wffn", [2, 4, NJG, 128, 6144])
    win_d = din("win", [2, 8, 128, 4096])
    wab_d = din("wab", [2, 128, 128])
    wout_d = din("wout", [2, 8, 128, 1024])
    wkva_d = din("wkva", [128, 8 * 384])
    wkvb_d = din("wkvb", [128, 2 * 2048])
    wdq_d = din("wdq", [2, 128, 8 * 384])
    wuq_d = din("wuq", [2, 128, 3 * 8 * 256])
    wo_d = din("wo", [2, 128, 8 * 1024])
    y_fm = nc.dram_tensor("y_fm", [NSEQ, 128, KC, S], F32, kind="ExternalOutput")
    scr_d = nc.dram_tensor("scr_rows", [3, 128, 128], F32, kind="Internal")
    dbg_out = {}

    from contextlib import ExitStack
    es = ExitStack()
    Hh = es.enter_context(nc.sbuf_tensor("H", [128, KC, S], F32))
    XNh = es.enter_context(nc.sbuf_tensor("XN", [128, KC, S], BF16))
    PERSh = es.enter_context(nc.sbuf_tensor("PERS", [128, 14336], BF16))
    AWh = es.enter_context(nc.sbuf_tensor("AW", [128, 36864], BF16))
    CONh = es.enter_context(nc.sbuf_tensor("CON", [128, NCONST], F32))
    GAINh = es.enter_context(nc.sbuf_tensor("GAIN", [128, 128], F32))
    CONVh = es.enter_context(nc.sbuf_tensor("CONVW", [128, 192], F32))
    CBh = es.enter_context(nc.sbuf_tensor("CONB", [128, 12 * 128], BF16))
    PSh = [es.enter_context(nc.psum_tensor("ps%d" % i, [128, 512], F32)) for i in range(8)]
    semnames = list(ENGS) + ["%s_d%d" % (e, i) for e in ("sp", "pool") for i in range(NDSEM)]
    sems = {n: es.enter_context(nc.semaphore(n)) for n in semnames}
    block = es.enter_context(nc.Block())

    AWf = AWh.bitcast(F32)
    PERSf = PERSh.bitcast(F32)

    def aw(off, n, key, shape=None, f32=False):
        if f32:
            ap = AWf[:, off // 2:(off + n) // 2]
        else:
            ap = AWh[:, off:off + n]
        if shape is not None:
            names = " ".join("a%d" % i for i in range(len(shape)))
            kw = {"a%d" % i: s for i, s in enumerate(shape[:-1])}
            ap = ap.rearrange("p (%s) -> p %s" % (names, names), **kw)
        return V(ap, [("aw", i) for i in range(off // 512, (off + n - 1) // 512 + 1)])

    def pers(off, n, key, shape=None, f32=False):
        if f32:
            ap = PERSf[:, off // 2:(off + n) // 2]
        else:
            ap = PERSh[:, off:off + n]
        if shape is not None:
            names = " ".join("a%d" % i for i in range(len(shape)))
            kw = {"a%d" % i: s for i, s in enumerate(shape[:-1])}
            ap = ap.rearrange("p (%s) -> p %s" % (names, names), **kw)
        return V(ap, [("pers", i) for i in range(off // 256, (off + n - 1) // 256 + 1)])

    def lo64(v):
        return V(v.ap[0:64], v.keys)

    def cf(name, n=128):
        return V(CONh[:, CI[name]:CI[name] + n], ["const"])

    CB_NAMES = ["ident", "ones", "bd16", "m32", "m64", "m128"]

    def cb(name):
        i = CB_NAMES.index(name)
        return V(CBh[:, i * 128:(i + 1) * 128], ["constb"])

    def cb4(name):
        i = CB_NAMES.index(name)
        return V(bass.AP(CBh, i * 128, [[12 * 128, 128], [0, 4], [1, 128]]), ["constb"])

    def Ht(k, t):
        return V(Hh[:, k, t * 512:(t + 1) * 512], [("H", k, t)])

    def XNt(k, t):
        return V(XNh[:, k, t * 512:(t + 1) * 512], [("XN", k, t)])

    def PS(b, n=512, dt=F32):
        if dt == F32:
            return V(PSh[b][:, 0:n], [("ps", b)])
        return V(PSh[b].bitcast(BF16)[:, 0:n], [("ps", b)])

    def gain(i, k):
        return V(GAINh[:, i * 8 + k:i * 8 + k + 1], ["gain"])

    def gaincol(c):
        return V(GAINh[:, c:c + 1], ["gain"])

    def mm(out, lhsT, rhs, start=True, stop=True):
        return P.op("pe", lambda e: e.matmul(out.ap, lhsT=lhsT.ap, rhs=rhs.ap, start=start, stop=stop),
                    reads=[lhsT, rhs], writes=[out])

    def tr(out, in_, ident):
        return P.op("pe", lambda e: e.transpose(out.ap, in_.ap, ident.ap), reads=[in_, ident], writes=[out])

    def act(out, in_, func, scale=1.0, bias=None, eng="act"):
        rd = [in_]
        kw = {}
        if isinstance(scale, V):
            rd.append(scale)
            kw["scale"] = scale.ap
        else:
            kw["scale"] = float(scale)
        if isinstance(bias, V):
            rd.append(bias)
            kw["bias"] = bias.ap
        elif bias is not None:
            kw["bias"] = float(bias)
        return P.op(eng, lambda e: e.activation(out=out.ap, in_=in_.ap, func=func, **kw), reads=rd, writes=[out])

    def ts(out, in0, s1, op0, s2=None, op1=None, eng="dve"):
        rd = [in0]
        a1 = s1.ap if isinstance(s1, V) else float(s1)
        if isinstance(s1, V):
            rd.append(s1)
        a2 = None
        if s2 is not None:
            a2 = s2.ap if isinstance(s2, V) else float(s2)
            if isinstance(s2, V):
                rd.append(s2)
        if op1 is None:
            return P.op(eng, lambda e: e.tensor_scalar(out=out.ap, in0=in0.ap, scalar1=a1, scalar2=None, op0=op0),
                        reads=rd, writes=[out])
        return P.op(eng, lambda e: e.tensor_scalar(out=out.ap, in0=in0.ap, scalar1=a1, scalar2=a2, op0=op0, op1=op1),
                    reads=rd, writes=[out])

    def stt(out, in0, sc, in1, op0, op1):
        rd = [in0, in1]
        a = sc.ap if isinstance(sc, V) else float(sc)
        if isinstance(sc, V):
            rd.append(sc)
        return P.op("dve", lambda e: e.scalar_tensor_tensor(out=out.ap, in0=in0.ap, scalar=a, in1=in1.ap,
                                                              op0=op0, op1=op1), reads=rd, writes=[out])

    def tt(out, in0, in1, op, eng="dve"):
        return P.op(eng, lambda e: e.tensor_tensor(out=out.ap, in0=in0.ap, in1=in1.ap, op=op),
                    reads=[in0, in1], writes=[out])

    def cp(out, in_, eng="dve"):
        if eng == "act":
            return act(out, in_, AF.Copy)
        return P.op(eng, lambda e: e.tensor_copy(out=out.ap, in_=in_.ap), reads=[in_], writes=[out])

    def memset(out, val, eng="dve"):
        return P.op(eng, lambda e: e.memset(out.ap, val), writes=[out])

    def load(sem, out, src_ap, eng="pool"):
        return P.dma(eng, sem, [lambda e: e.dma_start(out=out.ap, in_=src_ap)], writes=[out])

    dbg_list = []

    def dbg(name, v, shape, dt=F32):
        if not cfg.dbg:
            return
        t = nc.dram_tensor("dbg_" + name, list(shape), dt, kind="ExternalOutput")
        dbg_list.append("dbg_" + name)
        P.dma("sp", "dbg", [lambda e: e.dma_start(out=t.ap(), in_=v.ap)], reads=[v], writes=["dbgout"])

    P.dma("sp", "ld_const", [lambda e: e.dma_start(out=CONh[:], in_=consts_d.ap()),
                             lambda e: e.dma_start(out=GAINh[:], in_=gains_d.ap()),
                             lambda e: e.dma_start(out=CONVh[:], in_=convw_d.ap())],
          writes=["const", "gain", "convw"])
    for i, nme in enumerate(CB_NAMES):
        cp(V(CBh[:, i * 128:(i + 1) * 128], ["constb"]), cf(nme))
    P.barrier()

    def norm_tile(src_fn, dst_fn, sq_fn, nk, gain_fn, rs, lnt, dim, psb, post_scale=1.0, eps=True):
        for k in range(nk):
            act(sq_fn(k), src_fn(k), AF.Square)
        ps = PS(psb)
        for k in range(nk):
            mm(ps, cb("ones"), sq_fn(k), start=(k == 0), stop=(k == nk - 1))
        act(lnt, ps, AF.Ln, scale=1.0 / dim, bias=cf("eps", 1))
        act(rs, lnt, AF.Exp, scale=-0.5)
        for k in range(nk):
            g = gain_fn(k) if gain_fn is not None else post_scale
            stt(dst_fn(k), src_fn(k), g, rs, ALU.mult, ALU.mult)

    RS = aw(36864 - 2048, 1024, "rs", f32=True)
    LNT = aw(36864 - 1024, 1024, "lnt", f32=True)

    def norm_H_tile(gi, t, psb=6):
        norm_tile(lambda k: Ht(k, t), lambda k: XNt(k, t), lambda k: XNt(k, t), KC,
                  lambda k: gain(gi, k), RS, LNT, D, psb)

    def norm_H_to_XN(gi):
        for t in range(NT512):
            norm_H_tile(gi, t)

    def ffn_phase(f, l, gi, prenormed=False, on_final=None):
        if not prenormed:
            norm_H_to_XN(gi)
        NSLOT = 3
        WS = [aw(i * 6144, 6144, ("ws", i)) for i in range(NSLOT)]
        ACTB = [aw(18432 + i * 1024, 1024, ("actb", i), shape=[2, 512]) for i in range(2)]
        SG = [aw(20480 + i * 1024, 1024, ("sg", i), f32=True) for i in range(2)]

        def issue_load(jg):
            load("w%d" % (jg % NSLOT), WS[jg % NSLOT], wffn_d.ap()[f, l, jg])

        for jg in range(min(NSLOT, NJG)):
            issue_load(jg)
        cnt = [0]

        def down(jg, t, ab):
            w = WS[jg % NSLOT]
            for m in range(KC):
                ps = PS(4 + (m % 4))
                for j in range(2):
                    mm(ps, w[:, 4096 + j * 1024 + m * 128:4096 + j * 1024 + (m + 1) * 128], V(ab.ap[:, j, :], [ab.keys[j]]),
                       start=(j == 0), stop=(j == 1))
                stt(Ht(m, t), ps, 0.5, Ht(m, t), ALU.mult, ALU.add)
            if jg == NJG - 1 and on_final is not None:
                on_final(t)

        pending = None
        for jg in range(NJG):
            w = WS[jg % NSLOT]
            for t in range(NT512):
                ab = ACTB[cnt[0] % 2]
                cnt[0] += 1
                for j in range(2):
                    pg = PS(j)
                    pu = PS(2 + j)
                    for k in range(KC):
                        mm(pg, w[:, k * 512 + j * 128:k * 512 + (j + 1) * 128], XNt(k, t), start=(k == 0), stop=(k == KC - 1))
                    for k in range(KC):
                        mm(pu, w[:, k * 512 + 256 + j * 128:k * 512 + 256 + (j + 1) * 128], XNt(k, t),
                           start=(k == 0), stop=(k == KC - 1))
                    act(SG[j], pg, AF.Silu)
                    tt(V(ab.ap[:, j, :], [ab.keys[j]]), SG[j], pu, ALU.mult)
                if pending is not None:
                    down(*pending)
                pending = (jg, t, ab)
            if jg + NSLOT < NJG:
                down(*pending)
                pending = None
                issue_load(jg + NSLOT)
        if pending is not None:
            down(*pending)

    CKV = pers(0, 4096, "ckv", shape=[2, S])
    KR = pers(4096, 2048, "kr")
    CC = pers(6144, 4096, "cc", f32=True)
    SS = pers(10240, 4096, "ss", f32=True)

    def rope_tables(s):
        PI32 = V(AWh.bitcast(I32)[0:64, 0:2048], aw(0, 4096, None).keys)
        ANG = lo64(aw(4096, 4096, None, f32=True))
        T1 = lo64(aw(8192, 4096, None, f32=True))
        T2 = lo64(aw(12288, 4096, None, f32=True))
        KI = V(AWh.bitcast(I32)[0:64, 8192:10240], aw(16384, 4096, None).keys)
        P.dma("sp", "ld_x", [lambda e: e.dma_start(out=PI32.ap, in_=bass.AP(pos_d, s * S, [[0, 64], [1, S]]))],
              writes=[PI32])
        cp(ANG, PI32)
        ts(ANG, ANG, V(CONh[0:64, CI["inv"]:CI["inv"] + 1], ["const"]), ALU.mult)
        ts(T1, ANG, 1.0 / TWO_PI, ALU.mult)
        cp(KI, T1)
        cp(T1, KI)
        stt(T2, T1, -C1_2PI, ANG, ALU.mult, ALU.add)
        stt(T2, T1, -C2_2PI, T2, ALU.mult, ALU.add)
        PIC = 3.1415925
        ts(T1, T2, -PIC, ALU.max, PIC, ALU.min)
        sg = V(CONh[0:64, CI["sgn"]:CI["sgn"] + 1], ["const"])
        act(V(SS.ap[0:64], SS.keys), T1, AF.Sin, scale=sg)
        ts(T1, T2, float(np.pi / 2), ALU.is_gt, -TWO_PI, ALU.mult)
        stt(T1, T2, float(np.pi / 2), T1, ALU.add, ALU.add)
        ts(T1, T1, -PIC, ALU.max, PIC, ALU.min)
        act(V(CC.ap[0:64], CC.keys), T1, AF.Sin)

    def shared_kv_phase(s, prenormed=False, on_final=None):
        rope_tables(s)
        if not prenormed:
            norm_H_to_XN(12)
        WK = aw(0, 3072, "wkva", shape=[8, 384])
        RAWC = aw(4096, 2048, "rawc", shape=[2, 512], f32=True)
        SQC = aw(6144, 1024, "sqc", shape=[2, 512])
        T1 = lo64(aw(8192, 1024, None, f32=True))
        T2 = lo64(aw(9216, 1024, None, f32=True))
        load("wm", WK, wkva_d.ap())
        for t in range(NT512):
            tsl = slice(t * 512, (t + 1) * 512)
            for c in range(2):
                ps = PS(c)
                for k in range(KC):
                    mm(ps, WK[:, k, c * 128:(c + 1) * 128], XNt(k, t), start=(k == 0), stop=(k == KC - 1))
                cp(RAWC[:, c, :], ps, eng="act")
            norm_tile(lambda c: RAWC[:, c, :], lambda c: CKV[:, c, tsl], lambda c: SQC[:, c, :], 2,
                      lambda c: gaincol(104 + c), RS, LNT, 256, 6)
            px = V(PSh[2][0:64, :], [("ps", 2)])
            pxp = V(PSh[3][0:64, :], [("ps", 3)])
            for k in range(KC):
                mm(px, WK[:, k, 256:320], XNt(k, t), start=(k == 0), stop=(k == KC - 1))
            for k in range(KC):
                mm(pxp, WK[:, k, 320:384], XNt(k, t), start=(k == 0), stop=(k == KC - 1))
            tt(T1, px, V(CC.ap[0:64, tsl], CC.keys), ALU.mult)
            tt(T2, pxp, V(SS.ap[0:64, tsl], SS.keys), ALU.mult)
            tt(V(KR.ap[0:64, tsl], KR.keys), T1, T2, ALU.add)
            if on_final is not None:
                on_final(t)

    def mla_phase(l, prenormed=False, on_final=None):
        j = l - 2
        if not prenormed:
            norm_H_to_XN(4 + l)
        WUQ = aw(0, 6144, "wuq", shape=[3, 8, 256])
        WKB = aw(6144, 4096, "wkvb", shape=[2, 8, 256])
        CQ = aw(10240, 6144, "cq", shape=[3, S])
        WO = aw(0, 8192, None, shape=[8, 1024])
        T0 = 16384
        WDQ = aw(T0, 3072, "wdq", shape=[8, 384])
        CQR = aw(T0 + 3072, 3072, "cqr", shape=[3, 512], f32=True)
        SQQ = aw(T0 + 6144, 1536, "sqq", shape=[3, 512])
        load("wm", WDQ, wdq_d.ap()[j])
        load("w0", WUQ, wuq_d.ap()[j])
        load("w0", WKB, wkvb_d.ap())
        for t in range(NT512):
            tsl = slice(t * 512, (t + 1) * 512)
            for c in range(3):
                ps = PS(c)
                for k in range(KC):
                    mm(ps, WDQ[:, k, c * 128:(c + 1) * 128], XNt(k, t), start=(k == 0), stop=(k == KC - 1))
                cp(CQR[:, c, :], ps, eng="act")
            norm_tile(lambda c: CQR[:, c, :], lambda c: CQ[:, c, tsl], lambda c: SQQ[:, c, :], 3,
                      lambda c: gaincol(106 + j * 3 + c), RS, LNT, 384, 6)
        AO = lambda h, t: V(XNh[:, h, t * 512:(t + 1) * 512], [("XN", h, t)])
        scale = float((128 + 64) ** -0.5)
        NS = 2
        pools = [SlotPool(32 + si * 18, 18, "ms%d" % si) for si in range(NS)]
        remaining = [32]

        def stream(si):
            pool = pools[si]
            KTh = pool.alloc(2048)
            Vh = pool.alloc(2048, shape=[16, 128])
            PT = [pool.alloc(512) for _ in range(3)]
            ptc = 0
            for h in range(si, 8, NS):
                for t in range(NT512):
                    b = yield from acq()
                    ps = PS(b)
                    for c in range(2):
                        mm(ps, WKB[:, c, h, 0:128], CKV[:, c, t * 512:(t + 1) * 512], start=(c == 0), stop=(c == 1))
                    yield
                    cp(KTh[:, t * 512:(t + 1) * 512], ps, eng="act")
                    rel(b)
                for t4 in range(4):
                    b = yield from acq()
                    ps = PS(b)
                    for tq in range(4):
                        tk = t4 * 4 + tq
                        for c in range(2):
                            mm(ps[:, tq * 128:(tq + 1) * 128], CKV[:, c, tk * 128:(tk + 1) * 128], WKB[:, c, h, 128:256],
                               start=(c == 0), stop=(c == 1))
                    yield
                    cp(V(Vh.ap[:, t4 * 4:(t4 + 1) * 4, :].rearrange("p a b -> p (a b)"), Vh.keys), ps, eng="dve")
                    rel(b)
                for a in range(4):
                    gsl = slice(a * 512, (a + 1) * 512)
                    QN = pool.alloc(512)
                    QR = pool.alloc(512)
                    RT1 = pool.alloc(1024, f32=True)
                    RT2 = pool.alloc(1024, f32=True)
                    b = yield from acq()
                    ps = PS(b)
                    for c in range(3):
                        mm(ps, WUQ[:, c, h, 0:128], CQ[:, c, gsl], start=(c == 0), stop=(c == 2))
                    yield
                    cp(QN, ps, eng="act")
                    rel(b)
                    bx = yield from acq()
                    bxp = yield from acq()
                    px = V(PSh[bx][0:64, :], [("ps", bx)])
                    pxp = V(PSh[bxp][0:64, :], [("ps", bxp)])
                    for c in range(3):
                        mm(px, WUQ[:, c, h, 128:192], CQ[:, c, gsl], start=(c == 0), stop=(c == 2))
                    for c in range(3):
                        mm(pxp, WUQ[:, c, h, 192:256], CQ[:, c, gsl], start=(c == 0), stop=(c == 2))
                    remaining[0] -= 1
                    if remaining[0] == 0:
                        load("w1", WO, wo_d.ap()[j])
                    yield
                    r1 = V(RT1.ap[0:64], RT1.keys)
                    r2 = V(RT2.ap[0:64], RT2.keys)
                    tt(r1, px, V(CC.ap[0:64, gsl], CC.keys), ALU.mult)
                    rel(bx)
                    tt(r2, pxp, V(SS.ap[0:64, gsl], SS.keys), ALU.mult)
                    rel(bxp)
                    yield
                    tt(V(QR.ap[0:64], QR.keys), r1, r2, ALU.add)
                    pool.free(RT1, RT2)
                    ACC = pool.alloc(1024, f32=True)
                    bo = yield from acq()
                    pso = PS(bo)
                    nk = 4 * a + 4

                    def qk(jt):
                        c0 = max(0, jt - 4 * a) * 128
                        bst = yield from acq()
                        pst = PS(bst)
                        mm(pst[:, c0:], KTh[:, jt * 128:(jt + 1) * 128], QN[:, c0:], start=True, stop=False)
                        mm(pst[:, c0:], V(KR.ap[0:64, jt * 128:(jt + 1) * 128], KR.keys), V(QR.ap[0:64, c0:], QR.keys),
                           start=False, stop=True)
                        return bst, pst, c0
                    cur = yield from qk(0)
                    for jt in range(nk):
                        nxt_ = None
                        if jt + 1 < nk:
                            nxt_ = yield from qk(jt + 1)
                        bst, pst, c0 = cur
                        pt = PT[ptc % 3]
                        ptc += 1
                        yield
                        act(pt[:, c0:], pst[:, c0:], AF.Exp, scale=scale)
                        rel(bst)
                        if jt >= 4 * a:
                            memset(V(pt.ap[64:128, c0:c0 + 64], pt.keys), 0.0)
                        yield
                        mm(pso[:, c0:], Vh[:, jt, :], pt[:, c0:], start=(jt == 0), stop=(jt == nk - 1))
                        if jt == 0:
                            cp(ACC, pt)
                        else:
                            tt(ACC[:, c0:], ACC[:, c0:], pt[:, c0:], ALU.add)
                        cur = nxt_
                    pool.free(QN, QR)
                    LNS = pool.alloc(1024, f32=True)
                    RINV = pool.alloc(1024, f32=True)
                    bs = yield from acq()
                    pss = PS(bs)
                    mm(pss, cf("ones"), ACC)
                    yield
                    act(LNS, pss, AF.Ln)
                    rel(bs)
                    act(RINV, LNS, AF.Exp, scale=-1.0)
                    yield
                    tt(AO(h, a), pso, RINV, ALU.mult)
                    rel(bo)
                    pool.free(LNS, RINV, ACC)

        run_streams([stream(si) for si in range(NS)], offset=MLA_OFFSET)
        assert len(free_banks) == 8, free_banks
        for t in range(NT512):
            for m in range(KC):
                ps = PS(m % 2)
                for h in range(8):
                    mm(ps, WO[:, h, m * 128:(m + 1) * 128], AO(h, t), start=(h == 0), stop=(h == 7))
                tt(Ht(m, t), ps, Ht(m, t), ALU.add)
            if on_final is not None:
                on_final(t)

    class SlotPool:
        def __init__(self, base_slot, nslots, name):
            self.base = base_slot
            self.n = nslots
            self.name = name
            self.used = [False] * nslots
            self.peak = 0

        def alloc(self, nel, shape=None, f32=False):
            ns = (nel + 511) // 512
            for st in range(self.n - ns + 1):
                if not any(self.used[st:st + ns]):
                    for i in range(st, st + ns):
                        self.used[i] = True
                    self.peak = max(self.peak, sum(self.used))
                    v = aw((self.base + st) * 512, nel, None, shape=shape, f32=f32)
                    v.meta = (st, ns)
                    return v
            raise RuntimeError("slot pool %s exhausted (%d/%d used, need %d)" % (self.name, sum(self.used), self.n, ns))

        def free(self, *vs):
            for v in vs:
                st, ns = v.meta
                for i in range(st, st + ns):
                    assert self.used[i]
                    self.used[i] = False

    free_banks = list(range(8))

    def acq():
        while not free_banks:
            yield
        return free_banks.pop(0)

    def rel(*bs):
        for b in bs:
            free_banks.append(b)

    def run_streams(gens, offset=0):
        gens = list(gens)
        delay = {id(g): i * offset for i, g in enumerate(gens)}
        while gens:
            for g in list(gens):
                if delay[id(g)] > 0:
                    delay[id(g)] -= 1
                    continue
                try:
                    next(g)
                except StopIteration:
                    gens.remove(g)

    def norm_gen(src, dst, sq, rs, lnt, dim, gainv=None, post_scale=1.0):
        act(sq, src, AF.Square)
        b = yield from acq()
        ps = PS(b)
        mm(ps, cb("ones"), sq)
        yield
        act(lnt, ps, AF.Ln, scale=1.0 / dim, bias=cf("eps", 1))
        rel(b)
        act(rs, lnt, AF.Exp, scale=-0.5)
        yield
        stt(dst, src, gainv if gainv is not None else post_scale, rs, ALU.mult, ALU.mult)

    def gdn_phase(l, s, prenormed=False, on_final=None):
        if not prenormed:
            norm_H_to_XN(4 + l)
        li = l
        done_cnt = [0] * 4
        po = [0]

        def PA(key):
            v = pers(po[0], 256, key, f32=True)
            po[0] += 256
            return v
        G_, GC, LSB, SBv, NSB, EGC, EKD, EGL, C1, C2, KBS, TMPa, TMPb, ROWA, ROWD = [PA("sc%d" % i) for i in range(15)]
        WIN = [pers(3840 + i * 4096, 4096, ("win", i), shape=[8, 4, 128]) for i in range(2)]
        WAB = pers(3840 + 8192, 128, "wab", shape=[8, 16])
        load("wm", WAB, wab_d.ap()[li])
        P.dma("sp", "ld_x", [lambda e: e.dma_start(out=ROWA.ap, in_=bass.AP(rows_d, (li * 2 + 0) * 128, [[0, 128], [1, 128]])),
                             lambda e: e.dma_start(out=ROWD.ap, in_=bass.AP(rows_d, (li * 2 + 1) * 128, [[0, 128], [1, 128]]))],
              writes=[ROWA, ROWD])
        psab = PS(0, 256)
        for t in range(16):
            for k in range(KC):
                mm(psab[:, t * 16:(t + 1) * 16], V(XNh[:, k, t * 128:(t + 1) * 128], [("XN", k, t // 4)]), WAB[:, k, :],
                   start=(k == 0), stop=(k == KC - 1))
        ab3 = V(psab.ap.rearrange("p (t c) -> p t c", c=16), psab.keys)
        v3 = lambda v: V(v.ap.rearrange("p (h t) -> p t h", h=8), v.keys)
        tt(v3(TMPa), ab3[:, :, 0:8], v3(ROWD), ALU.add)
        act(TMPa, TMPa, AF.Exp)
        act(TMPa, TMPa, AF.Ln, bias=cf("one", 1))
        act(ROWA, ROWA, AF.Exp)
        stt(G_, TMPa, -1.0, ROWA, ALU.mult, ALU.mult)
        act(v3(TMPb), ab3[:, :, 8:16], AF.Exp, scale=-1.0)
        act(TMPb, TMPb, AF.Ln, bias=cf("one", 1))
        ts(LSB, TMPb, -0.5, ALU.mult)
        act(SBv, TMPb, AF.Exp, scale=-0.5)
        ts(NSB, SBv, -1.0, ALU.mult)
        psc = PS(1, 128)
        mm(psc, cf("tri"), G_)
        cp(GC, psc)
        psl = PS(2, 128)
        mm(psl, cf("ones"), G_)
        act(EGL, psl, AF.Exp)
        tt(TMPa, psl, GC, ALU.subtract)
        act(EKD, TMPa, AF.Exp)
        act(EGC, GC, AF.Exp)
        tt(C1, GC, LSB, ALU.add)
        tt(C2, GC, LSB, ALU.subtract)
        tt(KBS, SBv, EGC, ALU.mult)
        for vi, src in enumerate([C2, C1, GC]):
            pst = PS(3 + vi, 128)
            tr(pst, src, cf("ident"))
            dstv = [TMPa, TMPb, EGC][vi]
            cp(dstv, pst)
            P.dma("sp", None, [lambda e, vi=vi, dstv=dstv: e.dma_start(out=scr_d.ap()[vi], in_=dstv.ap)],
                  reads=[dstv], writes=[("scr", vi)])

        sc = lambda arr, col: V(arr.ap[:, col:col + 1], arr.keys)
        cw = lambda tap, chunk: V(CONVh[:, (li * 4 + tap) * 24 + chunk:(li * 4 + tap) * 24 + chunk + 1], ["convw"])
        dk_scale = float(128 ** -0.5)
        ident4 = cb4("ident")
        fl = lambda v: V(v.ap.rearrange("p a b -> p (a b)"), v.keys)
        NS = 2
        pools = [SlotPool(si * 34, 34, "gs%d" % si) for si in range(NS)]

        def stream(si):
            pool = pools[si]
            Wn = WIN[si]
            WOh = pool.alloc(1024)
            RSs = pool.alloc(1024, f32=True)
            LNs = pool.alloc(1024, f32=True)
            MISC = pool.alloc(512)
            HALO = V(MISC.ap[:, 0:16].rearrange("p (a b) -> p a b", a=4), MISC.keys)
            Sb = V(MISC.ap[:, 128:256], MISC.keys)
            VN = V(MISC.ap[:, 256:512].rearrange("p (a b) -> p a b", a=2), MISC.keys)
            DG = pool.alloc(1536, shape=[12, 128])
            Sst = pool.alloc(256, f32=True)
            heads = list(range(si, 8, NS))
            load("w%d" % si, Wn, win_d.ap()[li, heads[0]])
            load("w%d" % si, WOh, wout_d.ap()[li, heads[0]])
            its = [(hi, h, bi) for hi, h in enumerate(heads) for bi in range(4)]

            def stage1(hi, h, bi):
                if bi == 0:
                    for ci in range(3):
                        for tap in range(4):
                            act(DG[:, ci * 4 + tap, :], cb("ident"), AF.Copy, scale=cw(tap, ci * 8 + h))
                RAW = pool.alloc(1024)
                QT = pool.alloc(512)
                KT = pool.alloc(512)
                VTf = pool.alloc(512)
                GT = pool.alloc(512)
                dsts = [QT, KT, VTf]
                for ci in range(3):
                    b = yield from acq()
                    ps = PS(b)
                    for k in range(KC):
                        mm(ps, Wn[:, k, ci, :], XNt(k, bi), start=(k == 0), stop=(k == KC - 1))
                    if bi == 0:
                        memset(RAW[:, 0:3], 0.0)
                    else:
                        cp(RAW[:, 0:3], HALO[:, ci, 0:3])
                    yield
                    cp(RAW[:, 3:515], ps, eng="act")
                    rel(b)
                    yield
                    b2 = yield from acq()
                    pc = PS(b2)
                    for tap in range(4):
                        mm(pc, DG[:, ci * 4 + tap, :], RAW[:, tap:tap + 512], start=(tap == 0), stop=(tap == 3))
                    cp(HALO[:, ci, 0:3], RAW[:, 512:515])
                    yield
                    act(dsts[ci], pc, AF.Silu)
                    rel(b2)
                b = yield from acq()
                ps = PS(b)
                for k in range(KC):
                    mm(ps, Wn[:, k, 3, :], XNt(k, bi), start=(k == 0), stop=(k == KC - 1))
                if bi == 3 and hi + 1 < len(heads):
                    load("w%d" % si, Wn, win_d.ap()[li, heads[hi + 1]])
                yield
                act(GT, ps, AF.Silu)
                rel(b)
                pool.free(RAW)
                SQ = pool.alloc(512)
                yield from norm_gen(QT, QT, SQ, RSs, LNs, 1.0, post_scale=dk_scale)
                yield from norm_gen(KT, KT, SQ, RSs, LNs, 1.0, post_scale=1.0)
                pool.free(SQ)
                return QT, KT, VTf, GT

            state = {}

            def body(hi, h, bi, QT, KT, VTf, GT):
                if True:
                    if bi == 0:
                        memset(Sst, 0.0)
                        memset(Sb, 0.0)
                    X1 = pool.alloc(1024, shape=[4, 128], f32=True)
                    X2 = pool.alloc(1024, shape=[4, 128], f32=True)
                    X3 = pool.alloc(1024, shape=[4, 128], f32=True)
                    EGR = pool.alloc(512, shape=[4, 128])
                    col0 = h * 16 + bi * 4
                    for vi, Xv in enumerate([X1, X2, X3]):
                        P.dma("sp", None, [lambda e, vi=vi, Xv=Xv: e.dma_start(
                            out=fl(Xv).ap, in_=bass.AP(scr_d, vi * 16384 + col0 * 128, [[0, 128], [1, 512]]))],
                            reads=[("scr", vi)], writes=[Xv])
                    VTM = pool.alloc(512, shape=[4, 128])
                    KDEC = pool.alloc(512, shape=[4, 128])
                    KBG = pool.alloc(512, shape=[4, 128])
                    bk = yield from acq()
                    bv = yield from acq()
                    pk = PS(bk, 512, BF16)
                    pv = PS(bv, 512, BF16)
                    for t in range(4):
                        tr(pk[:, t * 128:(t + 1) * 128], KT[:, t * 128:(t + 1) * 128], cb("ident"))
                    for t in range(4):
                        tr(pv[:, t * 128:(t + 1) * 128], VTf[:, t * 128:(t + 1) * 128], cb("ident"))
                    yield
                    for t in range(4):
                        col = h * 16 + bi * 4 + t
                        ts(KDEC[:, t, :], pk[:, t * 128:(t + 1) * 128], sc(EKD, col), ALU.mult)
                        ts(KBG[:, t, :], pk[:, t * 128:(t + 1) * 128], sc(KBS, col), ALU.mult)
                    rel(bk)
                    yield
                    for t in range(4):
                        col = h * 16 + bi * 4 + t
                        ts(VTM[:, t, :], pv[:, t * 128:(t + 1) * 128], sc(SBv, col), ALU.mult)
                    rel(bv)
                    pool.free(VTf)
                    act(EGR, X3, AF.Exp)
                    for t in range(4):
                        col = h * 16 + bi * 4 + t
                        stt(X1[:, t, :], X1[:, t, :], sc(C1, col), cf("pos_ls"), ALU.subtract, ALU.add)
                        stt(X2[:, t, :], X2[:, t, :], sc(C2, col), cf("neg_us"), ALU.subtract, ALU.add)
                        stt(X3[:, t, :], X3[:, t, :], sc(GC, col), cf("neg_ui"), ALU.subtract, ALU.add)
                    DLs = pool.alloc(512, shape=[4, 128])
                    DUs = pool.alloc(512, shape=[4, 128])
                    DUi = pool.alloc(512, shape=[4, 128])
                    yield
                    act(DLs, X1, AF.Exp, scale=-1.0)
                    act(DUs, X2, AF.Exp)
                    act(DUi, X3, AF.Exp)
                    pool.free(X1, X2, X3)
                    bkk = yield from acq()
                    bqk = yield from acq()
                    pkk, pqk = PS(bkk), PS(bqk)
                    for t in range(4):
                        c4 = slice(t * 128, (t + 1) * 128)
                        mm(pkk[:, c4], KT[:, c4], KT[:, c4])
                        mm(pqk[:, c4], KT[:, c4], QT[:, c4])
                    yield
                    AL = pool.alloc(512, shape=[4, 128])
                    AU = pool.alloc(512, shape=[4, 128])
                    ATT = pool.alloc(512, shape=[4, 128])
                    QDT = pool.alloc(512, shape=[4, 128])
                    tt(fl(AL), pkk, fl(DLs), ALU.mult)
                    tt(fl(AU), pkk, fl(DUs), ALU.mult)
                    rel(bkk)
                    tt(fl(ATT), pqk, fl(DUi), ALU.mult)
                    rel(bqk)
                    tt(fl(QDT), QT, fl(EGR), ALU.mult)
                    pool.free(DLs, DUs, DUi, EGR, QT, KT)
                    state["go"] = True
                    yield
                    X0, X0T, R, RTr = [pool.alloc(512, shape=[4, 128]) for _ in range(4)]
                    Pa, PTa = [pool.alloc(512, shape=[4, 128]) for _ in range(2)]
                    Pb, PTb = X0, X0T
                    tt(X0, AL, cb4("bd16"), ALU.mult)
                    tt(X0T, AU, cb4("bd16"), ALU.mult)
                    tt(R, ident4, X0, ALU.subtract)
                    tt(RTr, ident4, X0T, ALU.subtract)
                    yield
                    Pc, PTc = X0, X0T
                    nxt = [(Pa, PTa), (Pb, PTb), (Pa, PTa)]

                    def mm4(ps, lh, rh):
                        for t in range(4):
                            mm(ps[:, t * 128:(t + 1) * 128], lh[:, t, :], rh[:, t, :])
                    for kk in range(3):
                        Pn, PTn = nxt[kk]
                        ba = yield from acq()
                        bb = yield from acq()
                        pa, pb = PS(ba), PS(bb)
                        mm4(pa, PTc, Pc)
                        mm4(pb, Pc, PTc)
                        yield
                        cp(fl(Pn), pa, eng="act")
                        cp(fl(PTn), pb, eng="act")
                        rel(ba, bb)
                        yield
                        ba = yield from acq()
                        bb = yield from acq()
                        pa2, pb2 = PS(ba), PS(bb)
                        mm4(pa2, PTn, R)
                        mm4(pb2, Pn, RTr)
                        yield
                        tt(fl(R), fl(R), pa2, ALU.add)
                        tt(fl(RTr), fl(RTr), pb2, ALU.add)
                        rel(ba, bb)
                        yield
                        Pc, PTc = Pn, PTn
                    pool.free(X0, X0T, Pa, PTa)
                    N_, NTr = R, RTr
                    OFF, OFFT, Z, ZT = [pool.alloc(512, shape=[4, 128]) for _ in range(4)]
                    NTT = pool.alloc(512, shape=[4, 128])
                    for lvl, mk in enumerate(["m32", "m64", "m128"]):
                        last = (lvl == 2)
                        tt(OFF, AL, cb4(mk), ALU.mult)
                        if not last:
                            tt(OFFT, AU, cb4(mk), ALU.mult)
                        yield
                        if not last:
                            bz = yield from acq()
                            pz = PS(bz)
                            mm4(pz, OFFT, N_)
                        bzt = yield from acq()
                        pzt = PS(bzt)
                        mm4(pzt, OFF, NTr)
                        yield
                        if not last:
                            cp(fl(Z), pz, eng="act")
                            rel(bz)
                        cp(fl(ZT), pzt, eng="act")
                        rel(bzt)
                        yield
                        if not last:
                            bw = yield from acq()
                            pw = PS(bw)
                            mm4(pw, NTr, Z)
                        bwt = yield from acq()
                        pwt = PS(bwt)
                        mm4(pwt, N_, ZT)
                        yield
                        if not last:
                            tt(fl(N_), fl(N_), pw, ALU.subtract)
                            rel(bw)
                            tt(fl(NTr), fl(NTr), pwt, ALU.subtract)
                        else:
                            tt(fl(NTT), fl(NTr), pwt, ALU.subtract)
                        rel(bwt)
                        yield
                    pool.free(OFF, OFFT, Z, ZT, R, RTr, AL, AU)
                    US = pool.alloc(1024, shape=[4, 128], f32=True)
                    WT = pool.alloc(512, shape=[4, 128])
                    bu = yield from acq()
                    bw_ = yield from acq()
                    pu, pw_ = PS(bu), PS(bw_)
                    for t in range(4):
                        c4 = slice(t * 128, (t + 1) * 128)
                        mm(pu[:, c4], NTT[:, t, :], VTM[:, t, :])
                        mm(pw_[:, c4], KBG[:, t, :], NTT[:, t, :])
                    yield
                    for t in range(4):
                        col = h * 16 + bi * 4 + t
                        ts(US[:, t, :], pu[:, t * 128:(t + 1) * 128], sc(SBv, col), ALU.mult)
                    rel(bu)
                    cp(fl(WT), pw_, eng="act")
                    rel(bw_)
                    pool.free(NTT, VTM, KBG)
                    yield
                    OT = pool.alloc(1024, f32=True)
                    for t in range(4):
                        col = h * 16 + bi * 4 + t
                        vn = VN[:, t % 2, :]
                        b1 = yield from acq()
                        p1 = PS(b1, 128)
                        mm(p1, WT[:, t, :], Sb)
                        yield
                        stt(vn, p1, sc(NSB, col), US[:, t, :], ALU.mult, ALU.add)
                        rel(b1)
                        yield
                        bo = yield from acq()
                        bd = yield from acq()
                        po_ = PS(bo, 128)
                        mm(po_, Sb, QDT[:, t, :], start=True, stop=False)
                        mm(po_, vn, ATT[:, t, :], start=False, stop=True)
                        pds = PS(bd, 128)
                        mm(pds, KDEC[:, t, :], vn)
                        yield
                        stt(Sst, Sst, sc(EGL, col), pds, ALU.mult, ALU.add)
                        rel(bd)
                        cp(OT[:, t * 128:(t + 1) * 128], po_, eng="act")
                        rel(bo)
                        yield
                        cp(Sb, Sst, eng="act")
                        yield
                    pool.free(US, WT, ATT, QDT, KDEC)
                    SQ = pool.alloc(512)
                    OG = pool.alloc(1024, f32=True)
                    OGb = pool.alloc(512)
                    yield from norm_gen(OT, OG, SQ, RSs, LNs, 128.0, gainv=gaincol(112 + li))
                    yield
                    tt(OGb, OG, GT, ALU.mult)
                    pool.free(SQ, OG, OT, GT)
                    yield
                    for m in range(KC):
                        b = yield from acq()
                        ps = PS(b)
                        mm(ps, WOh[:, m * 128:(m + 1) * 128], OGb)
                        yield
                        tt(Ht(m, bi), ps, Ht(m, bi), ALU.add)
                        rel(b)
                    pool.free(OGb)
                    if bi == 3 and hi + 1 < len(heads):
                        load("w%d" % si, WOh, wout_d.ap()[li, heads[hi + 1]])
                    if hi + 1 == len(heads):
                        done_cnt[bi] += 1
                        if done_cnt[bi] == NS and on_final is not None:
                            hb = free_banks.pop(0)
                            on_final(bi, hb)
                            free_banks.append(hb)
                    yield

            cur = yield from stage1(*its[0])
            for ii, it in enumerate(its):
                pre = stage1(*its[ii + 1]) if ii + 1 < len(its) else None
                pre_res = None
                state["go"] = False
                for _ in body(it[0], it[1], it[2], *cur):
                    yield
                    if pre is not None and state["go"]:
                        try:
                            next(pre)
                        except StopIteration as e:
                            pre_res = e.value
                            pre = None
                if pre is not None:
                    pre_res = yield from pre
                cur = pre_res

        run_streams([stream(si) for si in range(NS)], offset=GDN_OFFSET)
        assert len(free_banks) == 8, free_banks

    out_toks = []
    full = cfg.stop is None and cfg.phases is None

    def phase_gi(ph):
        return ph[3] if ph[0] == "ffn" else (12 if ph[0] == "kv" else 4 + ph[1])

    def final_tile(t, psb=6):
        norm_tile(lambda k: Ht(k, t), lambda k: Ht(k, t), lambda k: XNt(k, t), KC,
                  lambda k: gaincol(120 + k), RS, LNT, D, psb)

    for s in range(NSEQ):
        for t in range(NT512):
            P.dma("sp", "ld_x", [lambda e, s=s, k=k, t=t: e.dma_start(out=Hh[:, k, t * 512:(t + 1) * 512],
                                                                    in_=x_fm.ap()[s, :, k, t * 512:(t + 1) * 512])
                                 for k in range(KC)],
                  writes=[("H", k, t) for k in range(KC)])

        def store_tile(t, s=s):
            return P.dma("sp", "st_y", [lambda e, s=s, k=k, t=t: e.dma_start(out=y_fm.ap()[s, :, k, t * 512:(t + 1) * 512],
                                                                           in_=Hh[:, k, t * 512:(t + 1) * 512])
                                        for k in range(KC)],
                         reads=[("H", k, t) for k in range(KC)], writes=[("yout", s, t)])
        phases = []
        for l in range(4):
            phases.append(("ffn", 0, l, l))
            phases.append(("gdn", l) if l < 2 else ("mla", l))
            phases.append(("ffn", 1, l, 8 + l))
            if l == 1:
                phases.append(("kv",))
        if cfg.stop is not None:
            phases = phases[:cfg.stop]
        if cfg.phases is not None:
            phases = cfg.phases
        for pi, ph in enumerate(phases):
            if pi + 1 < len(phases):
                hook = (lambda t, psb=6, g=phase_gi(phases[pi + 1]): norm_H_tile(g, t, psb))
            else:
                hook = (lambda t, psb=6, st=store_tile: (final_tile(t, psb), st(t))) if full else None
            pre = pi > 0
            if ph[0] == "ffn":
                ffn_phase(ph[1], ph[2], ph[3], prenormed=pre, on_final=hook)
            elif ph[0] == "gdn":
                gdn_phase(ph[1], s, prenormed=pre, on_final=hook)
            elif ph[0] == "mla":
                mla_phase(ph[1], prenormed=pre, on_final=hook)
            else:
                shared_kv_phase(s, prenormed=pre, on_final=hook)
        if full and not phases:
            for t in range(NT512):
                final_tile(t)
        if not (full and phases):
            tok = P.dma("sp", "st_y", [lambda e, s=s, k=k: e.dma_start(out=y_fm.ap()[s, :, k, :], in_=Hh[:, k, :]) for k in range(KC)],
                        reads=[("H", k, t) for k in range(KC) for t in range(NT512)], writes=["yout"])
            out_toks.append(tok)
    P.wait_all("sp", [("dma:" + k, v) for k, v in P.dma_cnt.items() if k.startswith("sp_")])

    P.finalize(sems)

    @block.sync
    def _(e):
        P.emit("sp", e)

    @block.gpsimd
    def _(e):
        P.emit("pool", e)

    @block.tensor
    def _(e):
        P.emit("pe", e)

    @block.scalar
    def _(e):
        P.emit("act", e)

    @block.vector
    def _(e):
        P.emit("dve", e)

    es.close()
    return nc, dbg_list


def host_layout(inp, nseq_total):
    f = lambda a: np.ascontiguousarray(np.asarray(a, dtype=np.float32))
    out = {}
    x = f(inp["x"])
    out["x_fm"] = np.ascontiguousarray(x.transpose(0, 2, 1).reshape(x.shape[0], KC, 128, S).transpose(0, 2, 1, 3))
    out["pos"] = np.ascontiguousarray(np.asarray(inp["positions"], dtype=np.int32))
    g = np.zeros((128, 128), np.float32)
    vecs = [f(inp["ffn1_norm"])[l] for l in range(4)] + [f(inp["mix_norm"])[l] for l in range(4)] + \
           [f(inp["ffn2_norm"])[l] for l in range(4)] + [f(inp["kv_norm"])]
    for i, v in enumerate(vecs):
        g[:, i * 8:(i + 1) * 8] = v.reshape(8, 128).T
    g[:, 104:106] = f(inp["mla_kv_a_norm"]).reshape(2, 128).T
    for j in range(2):
        g[:, 106 + 3 * j:109 + 3 * j] = f(inp["mla_q_norm"])[j].reshape(3, 128).T
    g[:, 112:114] = f(inp["gdn_out_norm"]).T
    g[:, 120:128] = f(inp["final_norm"]).reshape(8, 128).T
    out["gains"] = g
    cw = f(inp["gdn_conv_w"])
    out["convw"] = np.ascontiguousarray(cw.reshape(2, 4, 24, 128).transpose(3, 0, 1, 2).reshape(128, 192))
    rows = np.zeros((2, 2, 128), np.float32)
    for i in range(2):
        rows[i, 0] = np.repeat(f(inp["gdn_a_log"])[i], 16)
        rows[i, 1] = np.repeat(f(inp["gdn_dt_bias"])[i], 16)
    out["rows"] = rows
    wffn = np.empty((2, 4, NJG, 128, 6144), np.float32)
    for fi, (gu, dn) in enumerate([("ffn1_w_gu", "ffn1_w_down"), ("ffn2_w_gu", "ffn2_w_down")]):
        wgu = f(inp[gu])
        wd = f(inp[dn])
        a = wgu.reshape(4, 8, 128, 2, NJG, 256)
        wffn[fi, :, :, :, :4096] = a.transpose(0, 4, 2, 1, 3, 5).reshape(4, NJG, 128, 4096)
        b = wd.reshape(4, NJG, 2, 128, 1024)
        wffn[fi, :, :, :, 4096:] = b.transpose(0, 1, 3, 2, 4).reshape(4, NJG, 128, 2048)
    out["wffn"] = wffn
    win = f(inp["gdn_w_in"])
    a = win[:, :, :4096].reshape(2, 8, 128, 4, 8, 128)
    out["win"] = np.ascontiguousarray(a.transpose(0, 4, 2, 1, 3, 5).reshape(2, 8, 128, 4096))
    out["wab"] = np.ascontiguousarray(win[:, :, 4096:].reshape(2, 8, 128, 16).transpose(0, 2, 1, 3).reshape(2, 128, 128))
    out["wout"] = np.ascontiguousarray(f(inp["gdn_w_out"]).reshape(2, 8, 128, 1024))
    perm = np.concatenate([np.arange(32, 64), np.arange(0, 32)])
    wkva = f(inp["mla_w_kv_a"])
    wk = np.concatenate([wkva, wkva[:, 256:320][:, perm]], axis=1)
    out["wkva"] = np.ascontiguousarray(wk.reshape(8, 128, 384).transpose(1, 0, 2).reshape(128, 8 * 384))
    out["wkvb"] = np.ascontiguousarray(f(inp["mla_w_kv_b"]).reshape(2, 128, 2048).transpose(1, 0, 2).reshape(128, 4096))
    out["wdq"] = np.ascontiguousarray(f(inp["mla_w_dq"]).reshape(2, 8, 128, 384).transpose(0, 2, 1, 3).reshape(2, 128, 3072))
    wuq = f(inp["mla_w_uq"]).reshape(2, 3, 128, 8, 192)
    wuqx = np.concatenate([wuq, wuq[..., 128:192][..., perm]], axis=-1)
    out["wuq"] = np.ascontiguousarray(wuqx.transpose(0, 2, 1, 3, 4).reshape(2, 128, 3 * 8 * 256))
    out["wo"] = np.ascontiguousarray(f(inp["mla_w_o"]).reshape(2, 8, 128, 1024).transpose(0, 2, 1, 3).reshape(2, 128, 8192))
    out["consts"] = CONSTS
    return out


def run(inputs, cfg, ncores=NCORES, batches=None):
    hl = host_layout(inputs, None)
    nc, dbg_list = build(cfg)
    B = hl["x_fm"].shape[0]
    if batches is None:
        batches = [list(range(c * cfg.nseq, (c + 1) * cfg.nseq)) for c in range(ncores)]
    in_maps = []
    shared = {k: v for k, v in hl.items() if k not in ("x_fm", "pos")}
    for bl in batches:
        m = dict(shared)
        m["x_fm"] = np.ascontiguousarray(hl["x_fm"][bl])
        m["pos"] = np.ascontiguousarray(hl["pos"][bl])
        in_maps.append(m)
    res = run_bass_kernel_spmd(nc, in_maps, core_ids=list(range(len(batches))))
    ys = []
    for r in res.results:
        y = r["y_fm"]
        ys.append(y.transpose(0, 3, 2, 1).reshape(y.shape[0], S, D))
    return np.concatenate(ys, axis=0), res, dbg_list


def kernel(**inputs):
    y, _, _ = run(inputs, Cfg(nseq=2))
    return np.ascontiguousarray(y.astype(np.float32))
```

```python
import numpy as np
import concourse.bass as bass
import concourse.mybir as mybir
from concourse.bass_utils import run_bass_kernel_spmd

F32 = mybir.dt.float32
BF16 = mybir.dt.bfloat16
I32 = mybir.dt.int32
AF = mybir.ActivationFunctionType
ALU = mybir.AluOpType

D = 1024
S = 2048
KC = 8
DFF = 2816
NJG = 11
NT512 = 4
EPS = 1e-6
BIG = 30000.0
NCORES = 8
TWO_PI = 2.0 * np.pi
C1_2PI = 6.28125
C2_2PI = float(TWO_PI - 6.28125)

CI = {}


def _build_consts():
    cols = []

    def add(name, arr):
        CI[name] = sum(a.shape[1] for a in cols)
        cols.append(arr.astype(np.float32))

    p = np.arange(128)[:, None]
    f = np.arange(128)[None, :]
    add("ident", (p == f))
    add("nident", -(p == f).astype(np.float32))
    add("ones", np.ones((128, 128)))
    add("tri", (p <= f))
    add("pos_ls", np.where(p > f, 0.0, BIG))
    add("neg_us", np.where(f > p, 0.0, -BIG))
    add("neg_ui", np.where(f >= p, 0.0, -BIG))
    bd = lambda b: ((p // b) == (f // b)).astype(np.float32)
    add("bd16", bd(16))
    add("m32", bd(32) - bd(16))
    add("m64", bd(64) - bd(32))
    add("m128", 1.0 - bd(64))
    half = 32
    inv = (10000.0 ** (-np.arange(half, dtype=np.float32) / half)).astype(np.float32)
    invp = np.zeros((128, 1), np.float32)
    invp[:64, 0] = np.concatenate([inv, inv])
    sgn = np.zeros((128, 1), np.float32)
    sgn[:32] = -1.0
    sgn[32:64] = 1.0
    add("inv", invp)
    add("sgn", sgn)
    add("eps", np.full((128, 1), EPS))
    add("one", np.full((128, 1), 1.0))
    return np.concatenate(cols, axis=1)


CONSTS = _build_consts()
NCONST = CONSTS.shape[1]


class V:
    __slots__ = ("ap", "keys", "meta")

    def __init__(self, ap, keys):
        self.ap = ap
        self.keys = tuple(keys)
        self.meta = None

    def __getitem__(self, idx):
        return V(self.ap[idx], self.keys)


class Op:
    __slots__ = ("fn", "deps", "inc", "semval", "waits", "real")

    def __init__(self, fn, deps, real=True):
        self.fn = fn
        self.deps = deps
        self.inc = False
        self.semval = 0
        self.waits = []
        self.real = real


ENGS = ("pe", "act", "dve", "pool", "sp")
SAME_ENGINE_SYNC = True
STRICT_SAME = True
NDSEM = 16
GDN_OFFSET = 13
MLA_OFFSET = 0


class Prog:
    def __init__(self, nc):
        self.nc = nc
        self.q = {e: [] for e in ENGS}
        self.wr = {}
        self.rd = {}
        self.dma_cnt = {}
        self.dma_rr = {}
        self.last_real = {}

    def reset_tracking(self):
        self.wr = {}
        self.rd = {}

    def _deps(self, reads, writes):
        deps = {}
        for r in reads:
            if r in self.wr:
                deps[self.wr[r]] = True
            if isinstance(r, tuple) and r[0] == "ps":
                for tok in self.rd.get(r, {}).values():
                    deps.setdefault(tok, False)
        for w in writes:
            if w in self.wr:
                deps[self.wr[w]] = deps.get(self.wr[w], False) or STRICT_SAME
            for tok in self.rd.get(w, {}).values():
                deps[tok] = deps.get(tok, False) or STRICT_SAME
        out = {}
        for (k, i), st in deps.items():
            t = (k, self.dma_cnt[k[4:]]) if k.startswith("dma:") else (k, i)
            out[t] = out.get(t, False) or st
        return out

    def _commit(self, tok, reads, writes, rkey):
        for w in writes:
            self.wr[w] = tok
            self.rd[w] = {}
        for r in reads:
            self.rd.setdefault(r, {})[rkey] = tok

    def op(self, eng, fn, reads=(), writes=()):
        reads = [k for v in reads for k in (v.keys if isinstance(v, V) else (v,))]
        writes = [k for v in writes for k in (v.keys if isinstance(v, V) else (v,))]
        deps = self._deps(reads, writes)
        o = Op(fn, deps)
        self.q[eng].append(o)
        tok = (eng, len(self.q[eng]) - 1)
        self.last_real[eng] = tok
        self._commit(tok, reads, writes, eng)
        return tok

    def dma(self, eng, sem, fns, reads=(), writes=()):
        reads = [k for v in reads for k in (v.keys if isinstance(v, V) else (v,))]
        writes = [k for v in writes for k in (v.keys if isinstance(v, V) else (v,))]
        deps = self._deps(reads, writes)
        rr = self.dma_rr.get(eng, 0)
        self.dma_rr[eng] = (rr + 1) % NDSEM
        sem = "%s_d%d" % (eng, rr)
        if sem in self.dma_cnt:
            deps[("dma:" + sem, self.dma_cnt[sem])] = True
        self.dma_cnt[sem] = self.dma_cnt.get(sem, 0) + 16 * len(fns)
        val = self.dma_cnt[sem]

        def fn(e, fns=fns, sem=sem):
            for f in fns:
                f(e).then_inc(self.sems[sem], 16)
            return None
        o = Op(fn, deps)
        self.q[eng].append(o)
        tok = ("dma:" + sem, val)
        self._commit(tok, reads, writes, "dma:" + sem)
        return tok

    def barrier(self):
        toks = dict(self.last_real)
        dtoks = [("dma:" + s, v) for s, v in self.dma_cnt.items()]
        for e in ENGS:
            deps = {t: True for k, t in toks.items() if k != e}
            deps.update({t: True for t in dtoks})
            self.q[e].append(Op(None, deps, real=False))
        self.reset_tracking()

    def wait_all(self, eng, toks):
        self.q[eng].append(Op(None, {t: True for t in toks}, real=False))

    def finalize(self, sems):
        self.sems = sems
        for e in ENGS:
            waited = {}
            for o in self.q[e]:
                best = {}
                for (k, idx), strong in o.deps.items():
                    if k == e and (e == "pe" or not strong or not SAME_ENGINE_SYNC):
                        continue
                    best[k] = max(best.get(k, -1), idx)
                for (k, idx) in sorted(best.items()):
                    if idx <= waited.get(k, -1 if not k.startswith("dma:") else 0):
                        continue
                    waited[k] = idx
                    if k.startswith("dma:"):
                        o.waits.append((k[4:], idx, None))
                    else:
                        tgt = self.q[k][idx]
                        assert tgt.real
                        tgt.inc = True
                        o.waits.append((k, None, tgt))
        for e in ENGS:
            c = 0
            for o in self.q[e]:
                if o.inc:
                    c += 1
                    o.semval = c

    def emit(self, eng_name, eng):
        for o in self.q[eng_name]:
            for (k, val, tgt) in o.waits:
                if tgt is None:
                    eng.wait_ge(self.sems[k], val)
                else:
                    eng.wait_ge(self.sems[k], tgt.semval)
            if o.fn is not None:
                ins = o.fn(eng)
                if o.inc:
                    ins.then_inc(self.sems[eng_name], 1)


class Cfg:
    def __init__(self, nseq=2, stop=None, dbg=False, phases=None):
        self.phases = phases
        self.nseq = nseq
        self.stop = stop
        self.dbg = dbg


def build(cfg):
    nc = bass.Bass("TRN2", target_bir_lowering=False)
    P = Prog(nc)
    NSEQ = cfg.nseq

    def din(name, shape, dt=F32):
        return nc.dram_tensor(name, list(shape), dt, kind="ExternalInput")

    x_fm = din("x_fm", [NSEQ, 128, KC, S])
    pos_d = din("pos", [NSEQ, S], I32)
    consts_d = din("consts", [128, NCONST])
    gains_d = din("gains", [128, 128])
    convw_d = din("convw", [128, 192])
    rows_d = din("rows", [2, 2, 128])
    wffn_d = din("wffn", [2, 4, NJG, 128, 6144])
    win_d = din("win", [2, 8, 128, 4096])
    wab_d = din("wab", [2, 128, 128])
    wout_d = din("wout", [2, 8, 128, 1024])
    wkva_d = din("wkva", [128, 8 * 384])
    wkvb_d = din("wkvb", [128, 2 * 2048])
    wdq_d = din("wdq", [2, 128, 8 * 384])
    wuq_d = din("wuq", [2, 128, 3 * 8 * 256])
    wo_d = din("wo", [2, 128, 8 * 1024])
    y_fm = nc.dram_tensor("y_fm", [NSEQ, 128, KC, S], F32, kind="ExternalOutput")
    scr_d = nc.dram_tensor("scr_rows", [3, 128, 128], F32, kind="Internal")
    dbg_out = {}

    from contextlib import ExitStack
    es = ExitStack()
    Hh = es.enter_context(nc.sbuf_tensor("H", [128, KC, S], F32))
    XNh = es.enter_context(nc.sbuf_tensor("XN", [128, KC, S], BF16))
    PERSh = es.enter_context(nc.sbuf_tensor("PERS", [128, 14336], BF16))
    AWh = es.enter_context(nc.sbuf_tensor("AW", [128, 36864], BF16))
    CONh = es.enter_context(nc.sbuf_tensor("CON", [128, NCONST], F32))
    GAINh = es.enter_context(nc.sbuf_tensor("GAIN", [128, 128], F32))
    CONVh = es.enter_context(nc.sbuf_tensor("CONVW", [128, 192], F32))
    CBh = es.enter_context(nc.sbuf_tensor("CONB", [128, 12 * 128], BF16))
    PSh = [es.enter_context(nc.psum_tensor("ps%d" % i, [128, 512], F32)) for i in range(8)]
    semnames = list(ENGS) + ["%s_d%d" % (e, i) for e in ("sp", "pool") for i in range(NDSEM)]
    sems = {n: es.enter_context(nc.semaphore(n)) for n in semnames}
    block = es.enter_context(nc.Block())

    AWf = AWh.bitcast(F32)
    PERSf = PERSh.bitcast(F32)

    def aw(off, n, key, shape=None, f32=False):
        if f32:
            ap = AWf[:, off // 2:(off + n) // 2]
        else:
            ap = AWh[:, off:off + n]
        if shape is not None:
            names = " ".join("a%d" % i for i in range(len(shape)))
            kw = {"a%d" % i: s for i, s in enumerate(shape[:-1])}
            ap = ap.rearrange("p (%s) -> p %s" % (names, names), **kw)
        return V(ap, [("aw", i) for i in range(off // 512, (off + n - 1) // 512 + 1)])

    def pers(off, n, key, shape=None, f32=False):
        if f32:
            ap = PERSf[:, off // 2:(off + n) // 2]
        else:
            ap = PERSh[:, off:off + n]
        if shape is not None:
            names = " ".join("a%d" % i for i in range(len(shape)))
            kw = {"a%d" % i: s for i, s in enumerate(shape[:-1])}
            ap = ap.rearrange("p (%s) -> p %s" % (names, names), **kw)
        return V(ap, [("pers", i) for i in range(off // 256, (off + n - 1) // 256 + 1)])

    def lo64(v):
        return V(v.ap[0:64], v.keys)

    def cf(name, n=128):
        return V(CONh[:, CI[name]:CI[name] + n], ["const"])

    CB_NAMES = ["ident", "ones", "bd16", "m32", "m64", "m128"]

    def cb(name):
        i = CB_NAMES.index(name)
        return V(CBh[:, i * 128:(i + 1) * 128], ["constb"])

    def cb4(name):
        i = CB_NAMES.index(name)
        return V(bass.AP(CBh, i * 128, [[12 * 128, 128], [0, 4], [1, 128]]), ["constb"])

    def Ht(k, t):
        return V(Hh[:, k, t * 512:(t + 1) * 512], [("H", k, t)])

    def XNt(k, t):
        return V(XNh[:, k, t * 512:(t + 1) * 512], [("XN", k, t)])

    def PS(b, n=512, dt=F32):
        if dt == F32:
            return V(PSh[b][:, 0:n], [("ps", b)])
        return V(PSh[b].bitcast(BF16)[:, 0:n], [("ps", b)])

    def gain(i, k):
        return V(GAINh[:, i * 8 + k:i * 8 + k + 1], ["gain"])

    def gaincol(c):
        return V(GAINh[:, c:c + 1], ["gain"])

    def mm(out, lhsT, rhs, start=True, stop=True):
        return P.op("pe", lambda e: e.matmul(out.ap, lhsT=lhsT.ap, rhs=rhs.ap, start=start, stop=stop),
                    reads=[lhsT, rhs], writes=[out])

    def tr(out, in_, ident):
        return P.op("pe", lambda e: e.transpose(out.ap, in_.ap, ident.ap), reads=[in_, ident], writes=[out])

    def act(out, in_, func, scale=1.0, bias=None, eng="act"):
        rd = [in_]
        kw = {}
        if isinstance(scale, V):
            rd.append(scale)
            kw["scale"] = scale.ap
        else:
            kw["scale"] = float(scale)
        if isinstance(bias, V):
            rd.append(bias)
            kw["bias"] = bias.ap
        elif bias is not None:
            kw["bias"] = float(bias)
        return P.op(eng, lambda e: e.activation(out=out.ap, in_=in_.ap, func=func, **kw), reads=rd, writes=[out])

    def ts(out, in0, s1, op0, s2=None, op1=None, eng="dve"):
        rd = [in0]
        a1 = s1.ap if isinstance(s1, V) else float(s1)
        if isinstance(s1, V):
            rd.append(s1)
        a2 = None
        if s2 is not None:
            a2 = s2.ap if isinstance(s2, V) else float(s2)
            if isinstance(s2, V):
                rd.append(s2)
        if op1 is None:
            return P.op(eng, lambda e: e.tensor_scalar(out=out.ap, in0=in0.ap, scalar1=a1, scalar2=None, op0=op0),
                        reads=rd, writes=[out])
        return P.op(eng, lambda e: e.tensor_scalar(out=out.ap, in0=in0.ap, scalar1=a1, scalar2=a2, op0=op0, op1=op1),
                    reads=rd, writes=[out])

    def stt(out, in0, sc, in1, op0, op1):
        rd = [in0, in1]
        a = sc.ap if isinstance(sc, V) else float(sc)
        if isinstance(sc, V):
            rd.append(sc)
        return P.op("dve", lambda e: e.scalar_tensor_tensor(out=out.ap, in0=in0.ap, scalar=a, in1=in1.ap,
                                                              op0=op0, op1=op1), reads=rd, writes=[out])

    def tt(out, in0, in1, op, eng="dve"):
        return P.op(eng, lambda e: e.tensor_tensor(out=out.ap, in0=in0.ap, in1=in1.ap, op=op),
                    reads=[in0, in1], writes=[out])

    def cp(out, in_, eng="dve"):
        if eng == "act":
            return act(out, in_, AF.Copy)
        return P.op(eng, lambda e: e.tensor_copy(out=out.ap, in_=in_.ap), reads=[in_], writes=[out])

    def memset(out, val, eng="dve"):
        return P.op(eng, lambda e: e.memset(out.ap, val), writes=[out])

    def load(sem, out, src_ap, eng="pool"):
        return P.dma(eng, sem, [lambda e: e.dma_start(out=out.ap, in_=src_ap)], writes=[out])

    dbg_list = []

    def dbg(name, v, shape, dt=F32):
        if not cfg.dbg:
            return
        t = nc.dram_tensor("dbg_" + name, list(shape), dt, kind="ExternalOutput")
        dbg_list.append("dbg_" + name)
        P.dma("sp", "dbg", [lambda e: e.dma_start(out=t.ap(), in_=v.ap)], reads=[v], writes=["dbgout"])

    P.dma("sp", "ld_const", [lambda e: e.dma_start(out=CONh[:], in_=consts_d.ap()),
                             lambda e: e.dma_start(out=GAINh[:], in_=gains_d.ap()),
                             lambda e: e.dma_start(out=CONVh[:], in_=convw_d.ap())],
          writes=["const", "gain", "convw"])
    for i, nme in enumerate(CB_NAMES):
        cp(V(CBh[:, i * 128:(i + 1) * 128], ["constb"]), cf(nme))
    P.barrier()

    def norm_tile(src_fn, dst_fn, sq_fn, nk, gain_fn, rs, lnt, dim, psb, post_scale=1.0, eps=True):
        for k in range(nk):
            act(sq_fn(k), src_fn(k), AF.Square)
        ps = PS(psb)
        for k in range(nk):
            mm(ps, cb("ones"), sq_fn(k), start=(k == 0), stop=(k == nk - 1))
        act(lnt, ps, AF.Ln, scale=1.0 / dim, bias=cf("eps", 1))
        act(rs, lnt, AF.Exp, scale=-0.5)
        for k in range(nk):
            g = gain_fn(k) if gain_fn is not None else post_scale
            stt(dst_fn(k), src_fn(k), g, rs, ALU.mult, ALU.mult)

    RS = aw(36864 - 2048, 1024, "rs", f32=True)
    LNT = aw(36864 - 1024, 1024, "lnt", f32=True)

    def norm_H_tile(gi, t, psb=6):
        norm_tile(lambda k: Ht(k, t), lambda k: XNt(k, t), lambda k: XNt(k, t), KC,
                  lambda k: gain(gi, k), RS, LNT, D, psb)

    def norm_H_to_XN(gi):
        for t in range(NT512):
            norm_H_tile(gi, t)

    def ffn_phase(f, l, gi, prenormed=False, on_final=None):
        if not prenormed:
            norm_H_to_XN(gi)
        NSLOT = 3
        WS = [aw(i * 6144, 6144, ("ws", i)) for i in range(NSLOT)]
        ACTB = [aw(18432 + i * 1024, 1024, ("actb", i), shape=[2, 512]) for i in range(2)]
        SG = [aw(20480 + i * 1024, 1024, ("sg", i), f32=True) for i in range(2)]

        def issue_load(jg):
            load("w%d" % (jg % NSLOT), WS[jg % NSLOT], wffn_d.ap()[f, l, jg])

        for jg in range(min(NSLOT, NJG)):
            issue_load(jg)
        cnt = [0]

        def down(jg, t, ab):
            w = WS[jg % NSLOT]
            for m in range(KC):
                ps = PS(4 + (m % 4))
                for j in range(2):
                    mm(ps, w[:, 4096 + j * 1024 + m * 128:4096 + j * 1024 + (m + 1) * 128], V(ab.ap[:, j, :], [ab.keys[j]]),
                       start=(j == 0), stop=(j == 1))
                stt(Ht(m, t), ps, 0.5, Ht(m, t), ALU.mult, ALU.add)
            if jg == NJG - 1 and on_final is not None:
                on_final(t)

        pending = None
        for jg in range(NJG):
            w = WS[jg % NSLOT]
            for t in range(NT512):
                ab = ACTB[cnt[0] % 2]
                cnt[0] += 1
                for j in range(2):
                    pg = PS(j)
                    pu = PS(2 + j)
                    for k in range(KC):
                        mm(pg, w[:, k * 512 + j * 128:k * 512 + (j + 1) * 128], XNt(k, t), start=(k == 0), stop=(k == KC - 1))
                    for k in range(KC):
                        mm(pu, w[:, k * 512 + 256 + j * 128:k * 512 + 256 + (j + 1) * 128], XNt(k, t),
                           start=(k == 0), stop=(k == KC - 1))
                    act(SG[j], pg, AF.Silu)
                    tt(V(ab.ap[:, j, :], [ab.keys[j]]), SG[j], pu, ALU.mult)
                if pending is not None:
                    down(*pending)
                pending = (jg, t, ab)
            if jg + NSLOT < NJG:
                down(*pending)
                pending = None
                issue_load(jg + NSLOT)
        if pending is not None:
            down(*pending)

    CKV = pers(0, 4096, "ckv", shape=[2, S])
    KR = pers(4096, 2048, "kr")
    CC = pers(6144, 4096, "cc", f32=True)
    SS = pers(10240, 4096, "ss", f32=True)

    def rope_tables(s):
        PI32 = V(AWh.bitcast(I32)[0:64, 0:2048], aw(0, 4096, None).keys)
        ANG = lo64(aw(4096, 4096, None, f32=True))
        T1 = lo64(aw(8192, 4096, None, f32=True))
        T2 = lo64(aw(12288, 4096, None, f32=True))
        KI = V(AWh.bitcast(I32)[0:64, 8192:10240], aw(16384, 4096, None).keys)
        P.dma("sp", "ld_x", [lambda e: e.dma_start(out=PI32.ap, in_=bass.AP(pos_d, s * S, [[0, 64], [1, S]]))],
              writes=[PI32])
        cp(ANG, PI32)
        ts(ANG, ANG, V(CONh[0:64, CI["inv"]:CI["inv"] + 1], ["const"]), ALU.mult)
        ts(T1, ANG, 1.0 / TWO_PI, ALU.mult)
        cp(KI, T1)
        cp(T1, KI)
        stt(T2, T1, -C1_2PI, ANG, ALU.mult, ALU.add)
        stt(T2, T1, -C2_2PI, T2, ALU.mult, ALU.add)
        PIC = 3.1415925
        ts(T1, T2, -PIC, ALU.max, PIC, ALU.min)
        sg = V(CONh[0:64, CI["sgn"]:CI["sgn"] + 1], ["const"])
        act(V(SS.ap[0:64], SS.keys), T1, AF.Sin, scale=sg)
        ts(T1, T2, float(np.pi / 2), ALU.is_gt, -TWO_PI, ALU.mult)
        stt(T1, T2, float(np.pi / 2), T1, ALU.add, ALU.add)
        ts(T1, T1, -PIC, ALU.max, PIC, ALU.min)
        act(V(CC.ap[0:64], CC.keys), T1, AF.Sin)

    def shared_kv_phase(s, prenormed=False, on_final=None):
        rope_tables(s)
        if not prenormed:
            norm_H_to_XN(12)
        WK = aw(0, 3072, "wkva", shape=[8, 384])
        RAWC = aw(4096, 2048, "rawc", shape=[2, 512], f32=True)
        SQC = aw(6144, 1024, "sqc", shape=[2, 512])
        T1 = lo64(aw(8192, 1024, None, f32=True))
        T2 = lo64(aw(9216, 1024, None, f32=True))
        load("wm", WK, wkva_d.ap())
        for t in range(NT512):
            tsl = slice(t * 512, (t + 1) * 512)
            for c in range(2):
                ps = PS(c)
                for k in range(KC):
                    mm(ps, WK[:, k, c * 128:(c + 1) * 128], XNt(k, t), start=(k == 0), stop=(k == KC - 1))
                cp(RAWC[:, c, :], ps, eng="act")
            norm_tile(lambda c: RAWC[:, c, :], lambda c: CKV[:, c, tsl], lambda c: SQC[:, c, :], 2,
                      lambda c: gaincol(104 + c), RS, LNT, 256, 6)
            px = V(PSh[2][0:64, :], [("ps", 2)])
            pxp = V(PSh[3][0:64, :], [("ps", 3)])
            for k in range(KC):
                mm(px, WK[:, k, 256:320], XNt(k, t), start=(k == 0), stop=(k == KC - 1))
            for k in range(KC):
                mm(pxp, WK[:, k, 320:384], XNt(k, t), start=(k == 0), stop=(k == KC - 1))
            tt(T1, px, V(CC.ap[0:64, tsl], CC.keys), ALU.mult)
            tt(T2, pxp, V(SS.ap[0:64, tsl], SS.keys), ALU.mult)
            tt(V(KR.ap[0:64, tsl], KR.keys), T1, T2, ALU.add)
            if on_final is not None:
                on_final(t)

    def mla_phase(l, prenormed=False, on_final=None):
        j = l - 2
        if not prenormed:
            norm_H_to_XN(4 + l)
        WUQ = aw(0, 6144, "wuq", shape=[3, 8, 256])
        WKB = aw(6144, 4096, "wkvb", shape=[2, 8, 256])
        CQ = aw(10240, 6144, "cq", shape=[3, S])
        WO = aw(0, 8192, None, shape=[8, 1024])
        T0 = 22528
        WDQ = aw(T0, 3072, "wdq", shape=[8, 384])
        CQR = aw(T0 + 3072, 3072, "cqr", shape=[3, 512], f32=True)
        SQQ = aw(T0 + 6144, 1536, "sqq", shape=[3, 512])
        load("wm", WDQ, wdq_d.ap()[j])
        load("w0", WUQ, wuq_d.ap()[j])
        load("w0", WKB, wkvb_d.ap())
        for t in range(NT512):
            tsl = slice(t * 512, (t + 1) * 512)
            for c in range(3):
                ps = PS(c)
                for k in range(KC):
                    mm(ps, WDQ[:, k, c * 128:(c + 1) * 128], XNt(k, t), start=(k == 0), stop=(k == KC - 1))
                cp(CQR[:, c, :], ps, eng="act")
            norm_tile(lambda c: CQR[:, c, :], lambda c: CQ[:, c, tsl], lambda c: SQQ[:, c, :], 3,
                      lambda c: gaincol(106 + j * 3 + c), RS, LNT, 384, 6)
        AO = lambda h, t: V(XNh[:, h, t * 512:(t + 1) * 512], [("XN", h, t)])
        scale = float((128 + 64) ** -0.5)
        NS = 2
        pools = [SlotPool(32 + si * 18, 18, "ms%d" % si) for si in range(NS)]
        remaining = [32]

        def stream(si):
            pool = pools[si]
            KTh = pool.alloc(2048)
            Vh = pool.alloc(2048, shape=[16, 128])
            PT = [pool.alloc(512) for _ in range(3)]
            ptc = 0
            for h in range(si, 8, NS):
                for t in range(NT512):
                    b = yield from acq()
                    ps = PS(b)
                    for c in range(2):
                        mm(ps, WKB[:, c, h, 0:128], CKV[:, c, t * 512:(t + 1) * 512], start=(c == 0), stop=(c == 1))
                    yield
                    cp(KTh[:, t * 512:(t + 1) * 512], ps, eng="act")
                    rel(b)
                for t4 in range(4):
                    b = yield from acq()
                    ps = PS(b)
                    for tq in range(4):
                        tk = t4 * 4 + tq
                        for c in range(2):
                            mm(ps[:, tq * 128:(tq + 1) * 128], CKV[:, c, tk * 128:(tk + 1) * 128], WKB[:, c, h, 128:256],
                               start=(c == 0), stop=(c == 1))
                    yield
                    cp(V(Vh.ap[:, t4 * 4:(t4 + 1) * 4, :].rearrange("p a b -> p (a b)"), Vh.keys), ps, eng="dve")
                    rel(b)
                for a in range(4):
                    gsl = slice(a * 512, (a + 1) * 512)
                    QN = pool.alloc(512)
                    QR = pool.alloc(512)
                    RT1 = pool.alloc(1024, f32=True)
                    RT2 = pool.alloc(1024, f32=True)
                    b = yield from acq()
                    ps = PS(b)
                    for c in range(3):
                        mm(ps, WUQ[:, c, h, 0:128], CQ[:, c, gsl], start=(c == 0), stop=(c == 2))
                    yield
                    cp(QN, ps, eng="act")
                    rel(b)
                    bx = yield from acq()
                    bxp = yield from acq()
                    px = V(PSh[bx][0:64, :], [("ps", bx)])
                    pxp = V(PSh[bxp][0:64, :], [("ps", bxp)])
                    for c in range(3):
                        mm(px, WUQ[:, c, h, 128:192], CQ[:, c, gsl], start=(c == 0), stop=(c == 2))
                    for c in range(3):
                        mm(pxp, WUQ[:, c, h, 192:256], CQ[:, c, gsl], start=(c == 0), stop=(c == 2))
                    remaining[0] -= 1
                    if remaining[0] == 0:
                        load("w1", WO, wo_d.ap()[j])
                    yield
                    r1 = V(RT1.ap[0:64], RT1.keys)
                    r2 = V(RT2.ap[0:64], RT2.keys)
                    tt(r1, px, V(CC.ap[0:64, gsl], CC.keys), ALU.mult)
                    rel(bx)
                    tt(r2, pxp, V(SS.ap[0:64, gsl], SS.keys), ALU.mult)
                    rel(bxp)
                    yield
                    tt(V(QR.ap[0:64], QR.keys), r1, r2, ALU.add)
                    pool.free(RT1, RT2)
                    ACC = pool.alloc(1024, f32=True)
                    bo = yield from acq()
                    pso = PS(bo)
                    nk = 4 * a + 4

                    def qk(jt):
                        c0 = max(0, jt - 4 * a) * 128
                        bst = yield from acq()
                        pst = PS(bst)
                        mm(pst[:, c0:], KTh[:, jt * 128:(jt + 1) * 128], QN[:, c0:], start=True, stop=False)
                        mm(pst[:, c0:], V(KR.ap[0:64, jt * 128:(jt + 1) * 128], KR.keys), V(QR.ap[0:64, c0:], QR.keys),
                           start=False, stop=True)
                        return bst, pst, c0
                    cur = yield from qk(0)
                    for jt in range(nk):
                        nxt_ = None
                        if jt + 1 < nk:
                            nxt_ = yield from qk(jt + 1)
                        bst, pst, c0 = cur
                        pt = PT[ptc % 3]
                        ptc += 1
                        yield
                        act(pt[:, c0:], pst[:, c0:], AF.Exp, scale=scale)
                        rel(bst)
                        if jt >= 4 * a:
                            memset(V(pt.ap[64:128, c0:c0 + 64], pt.keys), 0.0)
                        yield
                        mm(pso[:, c0:], Vh[:, jt, :], pt[:, c0:], start=(jt == 0), stop=(jt == nk - 1))
                        if jt == 0:
                            cp(ACC, pt)
                        else:
                            tt(ACC[:, c0:], ACC[:, c0:], pt[:, c0:], ALU.add)
                        cur = nxt_
                    pool.free(QN, QR)
                    LNS = pool.alloc(1024, f32=True)
                    RINV = pool.alloc(1024, f32=True)
                    bs = yield from acq()
                    pss = PS(bs)
                    mm(pss, cf("ones"), ACC)
                    yield
                    act(LNS, pss, AF.Ln)
                    rel(bs)
                    act(RINV, LNS, AF.Exp, scale=-1.0)
                    yield
                    tt(AO(h, a), pso, RINV, ALU.mult)
                    rel(bo)
                    pool.free(LNS, RINV, ACC)

        run_streams([stream(si) for si in range(NS)], offset=MLA_OFFSET)
        assert len(free_banks) == 8, free_banks
        for t in range(NT512):
            for m in range(KC):
                ps = PS(m % 2)
                for h in range(8):
                    mm(ps, WO[:, h, m * 128:(m + 1) * 128], AO(h, t), start=(h == 0), stop=(h == 7))
                tt(Ht(m, t), ps, Ht(m, t), ALU.add)
            if on_final is not None:
                on_final(t)

    class SlotPool:
        def __init__(self, base_slot, nslots, name):
            self.base = base_slot
            self.n = nslots
            self.name = name
            self.used = [False] * nslots
            self.peak = 0

        def alloc(self, nel, shape=None, f32=False):
            ns = (nel + 511) // 512
            for st in range(self.n - ns + 1):
                if not any(self.used[st:st + ns]):
                    for i in range(st, st + ns):
                        self.used[i] = True
                    self.peak = max(self.peak, sum(self.used))
                    v = aw((self.base + st) * 512, nel, None, shape=shape, f32=f32)
                    v.meta = (st, ns)
                    return v
            raise RuntimeError("slot pool %s exhausted (%d/%d used, need %d)" % (self.name, sum(self.used), self.n, ns))

        def free(self, *vs):
            for v in vs:
                st, ns = v.meta
                for i in range(st, st + ns):
                    assert self.used[i]
                    self.used[i] = False

    free_banks = list(range(8))

    def acq():
        while not free_banks:
            yield
        return free_banks.pop(0)

    def rel(*bs):
        for b in bs:
            free_banks.append(b)

    def run_streams(gens, offset=0):
        gens = list(gens)
        delay = {id(g): i * offset for i, g in enumerate(gens)}
        while gens:
            for g in list(gens):
                if delay[id(g)] > 0:
                    delay[id(g)] -= 1
                    continue
                try:
                    next(g)
                except StopIteration:
                    gens.remove(g)

    def norm_gen(src, dst, sq, rs, lnt, dim, gainv=None, post_scale=1.0):
        act(sq, src, AF.Square)
        b = yield from acq()
        ps = PS(b)
        mm(ps, cb("ones"), sq)
        yield
        act(lnt, ps, AF.Ln, scale=1.0 / dim, bias=cf("eps", 1))
        rel(b)
        act(rs, lnt, AF.Exp, scale=-0.5)
        yield
        stt(dst, src, gainv if gainv is not None else post_scale, rs, ALU.mult, ALU.mult)

    def gdn_phase(l, s, prenormed=False, on_final=None):
        if not prenormed:
            norm_H_to_XN(4 + l)
        li = l
        done_cnt = [0] * 4
        po = [0]

        def PA(key):
            v = pers(po[0], 256, key, f32=True)
            po[0] += 256
            return v
        G_, GC, LSB, SBv, NSB, EGC, EKD, EGL, C1, C2, KBS, TMPa, TMPb, ROWA, ROWD = [PA("sc%d" % i) for i in range(15)]
        WIN = [pers(3840 + i * 4096, 4096, ("win", i), shape=[8, 4, 128]) for i in range(2)]
        WAB = pers(3840 + 8192, 128, "wab", shape=[8, 16])
        load("wm", WAB, wab_d.ap()[li])
        P.dma("sp", "ld_x", [lambda e: e.dma_start(out=ROWA.ap, in_=bass.AP(rows_d, (li * 2 + 0) * 128, [[0, 128], [1, 128]])),
                             lambda e: e.dma_start(out=ROWD.ap, in_=bass.AP(rows_d, (li * 2 + 1) * 128, [[0, 128], [1, 128]]))],
              writes=[ROWA, ROWD])
        psab = PS(0, 256)
        for t in range(16):
            for k in range(KC):
                mm(psab[:, t * 16:(t + 1) * 16], V(XNh[:, k, t * 128:(t + 1) * 128], [("XN", k, t // 4)]), WAB[:, k, :],
                   start=(k == 0), stop=(k == KC - 1))
        ab3 = V(psab.ap.rearrange("p (t c) -> p t c", c=16), psab.keys)
        v3 = lambda v: V(v.ap.rearrange("p (h t) -> p t h", h=8), v.keys)
        tt(v3(TMPa), ab3[:, :, 0:8], v3(ROWD), ALU.add)
        act(TMPa, TMPa, AF.Exp)
        act(TMPa, TMPa, AF.Ln, bias=cf("one", 1))
        act(ROWA, ROWA, AF.Exp)
        stt(G_, TMPa, -1.0, ROWA, ALU.mult, ALU.mult)
        act(v3(TMPb), ab3[:, :, 8:16], AF.Exp, scale=-1.0)
        act(TMPb, TMPb, AF.Ln, bias=cf("one", 1))
        ts(LSB, TMPb, -0.5, ALU.mult)
        act(SBv, TMPb, AF.Exp, scale=-0.5)
        ts(NSB, SBv, -1.0, ALU.mult)
        psc = PS(1, 128)
        mm(psc, cf("tri"), G_)
        cp(GC, psc)
        psl = PS(2, 128)
        mm(psl, cf("ones"), G_)
        act(EGL, psl, AF.Exp)
        tt(TMPa, psl, GC, ALU.subtract)
        act(EKD, TMPa, AF.Exp)
        act(EGC, GC, AF.Exp)
        tt(C1, GC, LSB, ALU.add)
        tt(C2, GC, LSB, ALU.subtract)
        tt(KBS, SBv, EGC, ALU.mult)
        for vi, src in enumerate([C2, C1, GC]):
            pst = PS(3 + vi, 128)
            tr(pst, src, cf("ident"))
            dstv = [TMPa, TMPb, EGC][vi]
            cp(dstv, pst)
            P.dma("sp", None, [lambda e, vi=vi, dstv=dstv: e.dma_start(out=scr_d.ap()[vi], in_=dstv.ap)],
                  reads=[dstv], writes=[("scr", vi)])

        sc = lambda arr, col: V(arr.ap[:, col:col + 1], arr.keys)
        cw = lambda tap, chunk: V(CONVh[:, (li * 4 + tap) * 24 + chunk:(li * 4 + tap) * 24 + chunk + 1], ["convw"])
        dk_scale = float(128 ** -0.5)
        ident4 = cb4("ident")
        fl = lambda v: V(v.ap.rearrange("p a b -> p (a b)"), v.keys)
        NS = 2
        pools = [SlotPool(si * 34, 34, "gs%d" % si) for si in range(NS)]

        def stream(si):
            pool = pools[si]
            Wn = WIN[si]
            WOh = pool.alloc(1024)
            RSs = pool.alloc(1024, f32=True)
            LNs = pool.alloc(1024, f32=True)
            MISC = pool.alloc(512)
            HALO = V(MISC.ap[:, 0:16].rearrange("p (a b) -> p a b", a=4), MISC.keys)
            Sb = V(MISC.ap[:, 128:256], MISC.keys)
            VN = V(MISC.ap[:, 256:512].rearrange("p (a b) -> p a b", a=2), MISC.keys)
            DG = pool.alloc(1536, shape=[12, 128])
            Sst = pool.alloc(256, f32=True)
            heads = list(range(si, 8, NS))
            load("w%d" % si, Wn, win_d.ap()[li, heads[0]])
            load("w%d" % si, WOh, wout_d.ap()[li, heads[0]])
            its = [(hi, h, bi) for hi, h in enumerate(heads) for bi in range(4)]

            def stage1(hi, h, bi):
                if bi == 0:
                    for ci in range(3):
                        for tap in range(4):
                            act(DG[:, ci * 4 + tap, :], cb("ident"), AF.Copy, scale=cw(tap, ci * 8 + h))
                RAW = pool.alloc(1024)
                QT = pool.alloc(512)
                KT = pool.alloc(512)
                VTf = pool.alloc(512)
                GT = pool.alloc(512)
                dsts = [QT, KT, VTf]
                for ci in range(3):
                    b = yield from acq()
                    ps = PS(b)
                    for k in range(KC):
                        mm(ps, Wn[:, k, ci, :], XNt(k, bi), start=(k == 0), stop=(k == KC - 1))
                    if bi == 0:
                        memset(RAW[:, 0:3], 0.0)
                    else:
                        cp(RAW[:, 0:3], HALO[:, ci, 0:3])
                    yield
                    cp(RAW[:, 3:515], ps, eng="act")
                    rel(b)
                    yield
                    b2 = yield from acq()
                    pc = PS(b2)
                    for tap in range(4):
                        mm(pc, DG[:, ci * 4 + tap, :], RAW[:, tap:tap + 512], start=(tap == 0), stop=(tap == 3))
                    cp(HALO[:, ci, 0:3], RAW[:, 512:515])
                    yield
                    act(dsts[ci], pc, AF.Silu)
                    rel(b2)
                b = yield from acq()
                ps = PS(b)
                for k in range(KC):
                    mm(ps, Wn[:, k, 3, :], XNt(k, bi), start=(k == 0), stop=(k == KC - 1))
                if bi == 3 and hi + 1 < len(heads):
                    load("w%d" % si, Wn, win_d.ap()[li, heads[hi + 1]])
                yield
                act(GT, ps, AF.Silu)
                rel(b)
                pool.free(RAW)
                SQ = pool.alloc(512)
                yield from norm_gen(QT, QT, SQ, RSs, LNs, 1.0, post_scale=dk_scale)
                yield from norm_gen(KT, KT, SQ, RSs, LNs, 1.0, post_scale=1.0)
                pool.free(SQ)
                return QT, KT, VTf, GT

            state = {}

            def body(hi, h, bi, QT, KT, VTf, GT):
                if True:
                    if bi == 0:
                        memset(Sst, 0.0)
                        memset(Sb, 0.0)
                    X1 = pool.alloc(1024, shape=[4, 128], f32=True)
                    X2 = pool.alloc(1024, shape=[4, 128], f32=True)
                    X3 = pool.alloc(1024, shape=[4, 128], f32=True)
                    EGR = pool.alloc(512, shape=[4, 128])
                    col0 = h * 16 + bi * 4
                    for vi, Xv in enumerate([X1, X2, X3]):
                        P.dma("sp", None, [lambda e, vi=vi, Xv=Xv: e.dma_start(
                            out=fl(Xv).ap, in_=bass.AP(scr_d, vi * 16384 + col0 * 128, [[0, 128], [1, 512]]))],
                            reads=[("scr", vi)], writes=[Xv])
                    VTM = pool.alloc(512, shape=[4, 128])
                    KDEC = pool.alloc(512, shape=[4, 128])
                    KBG = pool.alloc(512, shape=[4, 128])
                    bk = yield from acq()
                    bv = yield from acq()
                    pk = PS(bk, 512, BF16)
                    pv = PS(bv, 512, BF16)
                    for t in range(4):
                        tr(pk[:, t * 128:(t + 1) * 128], KT[:, t * 128:(t + 1) * 128], cb("ident"))
                    for t in range(4):
                        tr(pv[:, t * 128:(t + 1) * 128], VTf[:, t * 128:(t + 1) * 128], cb("ident"))
                    yield
                    for t in range(4):
                        col = h * 16 + bi * 4 + t
                        ts(KDEC[:, t, :], pk[:, t * 128:(t + 1) * 128], sc(EKD, col), ALU.mult)
                        ts(KBG[:, t, :], pk[:, t * 128:(t + 1) * 128], sc(KBS, col), ALU.mult)
                    rel(bk)
                    yield
                    for t in range(4):
                        col = h * 16 + bi * 4 + t
                        ts(VTM[:, t, :], pv[:, t * 128:(t + 1) * 128], sc(SBv, col), ALU.mult)
                    rel(bv)
                    pool.free(VTf)
                    act(EGR, X3, AF.Exp)
                    for t in range(4):
                        col = h * 16 + bi * 4 + t
                        stt(X1[:, t, :], X1[:, t, :], sc(C1, col), cf("pos_ls"), ALU.subtract, ALU.add)
                        stt(X2[:, t, :], X2[:, t, :], sc(C2, col), cf("neg_us"), ALU.subtract, ALU.add)
                        stt(X3[:, t, :], X3[:, t, :], sc(GC, col), cf("neg_ui"), ALU.subtract, ALU.add)
                    DLs = pool.alloc(512, shape=[4, 128])
                    DUs = pool.alloc(512, shape=[4, 128])
                    DUi = pool.alloc(512, shape=[4, 128])
                    yield
                    act(DLs, X1, AF.Exp, scale=-1.0)
                    act(DUs, X2, AF.Exp)
                    act(DUi, X3, AF.Exp)
                    pool.free(X1, X2, X3)
                    bkk = yield from acq()
                    bqk = yield from acq()
                    pkk, pqk = PS(bkk), PS(bqk)
                    for t in range(4):
                        c4 = slice(t * 128, (t + 1) * 128)
                        mm(pkk[:, c4], KT[:, c4], KT[:, c4])
                        mm(pqk[:, c4], KT[:, c4], QT[:, c4])
                    yield
                    AL = pool.alloc(512, shape=[4, 128])
                    AU = pool.alloc(512, shape=[4, 128])
                    ATT = pool.alloc(512, shape=[4, 128])
                    QDT = pool.alloc(512, shape=[4, 128])
                    tt(fl(AL), pkk, fl(DLs), ALU.mult)
                    tt(fl(AU), pkk, fl(DUs), ALU.mult)
                    rel(bkk)
                    tt(fl(ATT), pqk, fl(DUi), ALU.mult)
                    rel(bqk)
                    tt(fl(QDT), QT, fl(EGR), ALU.mult)
                    pool.free(DLs, DUs, DUi, EGR, QT, KT)
                    state["go"] = True
                    yield
                    X0, X0T, R, RTr = [pool.alloc(512, shape=[4, 128]) for _ in range(4)]
                    Pa, PTa = [pool.alloc(512, shape=[4, 128]) for _ in range(2)]
                    Pb, PTb = X0, X0T
                    tt(X0, AL, cb4("bd16"), ALU.mult)
                    tt(X0T, AU, cb4("bd16"), ALU.mult)
                    tt(R, ident4, X0, ALU.subtract)
                    tt(RTr, ident4, X0T, ALU.subtract)
                    yield
                    Pc, PTc = X0, X0T
                    nxt = [(Pa, PTa), (Pb, PTb), (Pa, PTa)]

                    def mm4(ps, lh, rh):
                        for t in range(4):
                            mm(ps[:, t * 128:(t + 1) * 128], lh[:, t, :], rh[:, t, :])
                    for kk in range(3):
                        Pn, PTn = nxt[kk]
                        ba = yield from acq()
                        bb = yield from acq()
                        pa, pb = PS(ba), PS(bb)
                        mm4(pa, PTc, Pc)
                        mm4(pb, Pc, PTc)
                        yield
                        cp(fl(Pn), pa, eng="act")
                        cp(fl(PTn), pb, eng="act")
                        rel(ba, bb)
                        yield
                        ba = yield from acq()
                        bb = yield from acq()
                        pa2, pb2 = PS(ba), PS(bb)
                        mm4(pa2, PTn, R)
                        mm4(pb2, Pn, RTr)
                        yield
                        tt(fl(R), fl(R), pa2, ALU.add)
                        tt(fl(RTr), fl(RTr), pb2, ALU.add)
                        rel(ba, bb)
                        yield
                        Pc, PTc = Pn, PTn
                    pool.free(X0, X0T, Pa, PTa)
                    N_, NTr = R, RTr
                    OFF, OFFT, Z, ZT = [pool.alloc(512, shape=[4, 128]) for _ in range(4)]
                    NTT = pool.alloc(512, shape=[4, 128])
                    for lvl, mk in enumerate(["m32", "m64", "m128"]):
                        last = (lvl == 2)
                        tt(OFF, AL, cb4(mk), ALU.mult)
                        if not last:
                            tt(OFFT, AU, cb4(mk), ALU.mult)
                        yield
                        if not last:
                            bz = yield from acq()
                            pz = PS(bz)
                            mm4(pz, OFFT, N_)
                        bzt = yield from acq()
                        pzt = PS(bzt)
                        mm4(pzt, OFF, NTr)
                        yield
                        if not last:
                            cp(fl(Z), pz, eng="act")
                            rel(bz)
                        cp(fl(ZT), pzt, eng="act")
                        rel(bzt)
                        yield
                        if not last:
                            bw = yield from acq()
                            pw = PS(bw)
                            mm4(pw, NTr, Z)
                        bwt = yield from acq()
                        pwt = PS(bwt)
                        mm4(pwt, N_, ZT)
                        yield
                        if not last:
                            tt(fl(N_), fl(N_), pw, ALU.subtract)
                            rel(bw)
                            tt(fl(NTr), fl(NTr), pwt, ALU.subtract)
                        else:
                            tt(fl(NTT), fl(NTr), pwt, ALU.subtract)
                        rel(bwt)
                        yield
                    pool.free(OFF, OFFT, Z, ZT, R, RTr, AL, AU)
                    US = pool.alloc(1024, shape=[4, 128], f32=True)
                    WT = pool.alloc(512, shape=[4, 128])
                    bu = yield from acq()
                    bw_ = yield from acq()
                    pu, pw_ = PS(bu), PS(bw_)
                    for t in range(4):
                        c4 = slice(t * 128, (t + 1) * 128)
                        mm(pu[:, c4], NTT[:, t, :], VTM[:, t, :])
                        mm(pw_[:, c4], KBG[:, t, :], NTT[:, t, :])
                    yield
                    for t in range(4):
                        col = h * 16 + bi * 4 + t
                        ts(US[:, t, :], pu[:, t * 128:(t + 1) * 128], sc(SBv, col), ALU.mult)
                    rel(bu)
                    cp(fl(WT), pw_, eng="act")
                    rel(bw_)
                    pool.free(NTT, VTM, KBG)
                    yield
                    OT = pool.alloc(1024, f32=True)
                    for t in range(4):
                        col = h * 16 + bi * 4 + t
                        vn = VN[:, t % 2, :]
                        b1 = yield from acq()
                        p1 = PS(b1, 128)
                        mm(p1, WT[:, t, :], Sb)
                        yield
                        stt(vn, p1, sc(NSB, col), US[:, t, :], ALU.mult, ALU.add)
                        rel(b1)
                        yield
                        bo = yield from acq()
                        bd = yield from acq()
                        po_ = PS(bo, 128)
                        mm(po_, Sb, QDT[:, t, :], start=True, stop=False)
                        mm(po_, vn, ATT[:, t, :], start=False, stop=True)
                        pds = PS(bd, 128)
                        mm(pds, KDEC[:, t, :], vn)
                        yield
                        stt(Sst, Sst, sc(EGL, col), pds, ALU.mult, ALU.add)
                        rel(bd)
                        cp(OT[:, t * 128:(t + 1) * 128], po_, eng="act")
                        rel(bo)
                        yield
                        cp(Sb, Sst, eng="act")
                        yield
                    pool.free(US, WT, ATT, QDT, KDEC)
                    SQ = pool.alloc(512)
                    OG = pool.alloc(1024, f32=True)
                    OGb = pool.alloc(512)
                    yield from norm_gen(OT, OG, SQ, RSs, LNs, 128.0, gainv=gaincol(112 + li))
                    yield
                    tt(OGb, OG, GT, ALU.mult)
                    pool.free(SQ, OG, OT, GT)
                    yield
                    for m in range(KC):
                        b = yield from acq()
                        ps = PS(b)
                        mm(ps, WOh[:, m * 128:(m + 1) * 128], OGb)
                        yield
                        tt(Ht(m, bi), ps, Ht(m, bi), ALU.add)
                        rel(b)
                    pool.free(OGb)
                    if bi == 3 and hi + 1 < len(heads):
                        load("w%d" % si, WOh, wout_d.ap()[li, heads[hi + 1]])
                    if hi + 1 == len(heads):
                        done_cnt[bi] += 1
                        if done_cnt[bi] == NS and on_final is not None:
                            hb = free_banks.pop(0)
                            on_final(bi, hb)
                            free_banks.append(hb)
                    yield

            cur = yield from stage1(*its[0])
            for ii, it in enumerate(its):
                pre = stage1(*its[ii + 1]) if ii + 1 < len(its) else None
                pre_res = None
                state["go"] = False
                for _ in body(it[0], it[1], it[2], *cur):
                    yield
                    if pre is not None and state["go"]:
                        try:
                            next(pre)
                        except StopIteration as e:
                            pre_res = e.value
                            pre = None
                if pre is not None:
                    pre_res = yield from pre
                cur = pre_res

        run_streams([stream(si) for si in range(NS)], offset=GDN_OFFSET)
        assert len(free_banks) == 8, free_banks

    out_toks = []
    full = cfg.stop is None and cfg.phases is None

    def phase_gi(ph):
        return ph[3] if ph[0] == "ffn" else (12 if ph[0] == "kv" else 4 + ph[1])

    def final_tile(t, psb=6):
        norm_tile(lambda k: Ht(k, t), lambda k: Ht(k, t), lambda k: XNt(k, t), KC,
                  lambda k: gaincol(120 + k), RS, LNT, D, psb)

    for s in range(NSEQ):
        for t in range(NT512):
            P.dma("sp", "ld_x", [lambda e, s=s, k=k, t=t: e.dma_start(out=Hh[:, k, t * 512:(t + 1) * 512],
                                                                    in_=x_fm.ap()[s, :, k, t * 512:(t + 1) * 512])
                                 for k in range(KC)],
                  writes=[("H", k, t) for k in range(KC)])

        def store_tile(t, s=s):
            return P.dma("sp", "st_y", [lambda e, s=s, k=k, t=t: e.dma_start(out=y_fm.ap()[s, :, k, t * 512:(t + 1) * 512],
                                                                           in_=Hh[:, k, t * 512:(t + 1) * 512])
                                        for k in range(KC)],
                         reads=[("H", k, t) for k in range(KC)], writes=[("yout", s, t)])
        phases = []
        for l in range(4):
            phases.append(("ffn", 0, l, l))
            phases.append(("gdn", l) if l < 2 else ("mla", l))
            phases.append(("ffn", 1, l, 8 + l))
            if l == 1:
                phases.append(("kv",))
        if cfg.stop is not None:
            phases = phases[:cfg.stop]
        if cfg.phases is not None:
            phases = cfg.phases
        for pi, ph in enumerate(phases):
            if pi + 1 < len(phases):
                hook = (lambda t, psb=6, g=phase_gi(phases[pi + 1]): norm_H_tile(g, t, psb))
            else:
                hook = (lambda t, psb=6, st=store_tile: (final_tile(t, psb), st(t))) if full else None
            pre = pi > 0
            if ph[0] == "ffn":
                ffn_phase(ph[1], ph[2], ph[3], prenormed=pre, on_final=hook)
            elif ph[0] == "gdn":
                gdn_phase(ph[1], s, prenormed=pre, on_final=hook)
            elif ph[0] == "mla":
                mla_phase(ph[1], prenormed=pre, on_final=hook)
            else:
                shared_kv_phase(s, prenormed=pre, on_final=hook)
        if full and not phases:
            for t in range(NT512):
                final_tile(t)
        if not (full and phases):
            tok = P.dma("sp", "st_y", [lambda e, s=s, k=k: e.dma_start(out=y_fm.ap()[s, :, k, :], in_=Hh[:, k, :]) for k in range(KC)],
                        reads=[("H", k, t) for k in range(KC) for t in range(NT512)], writes=["yout"])
            out_toks.append(tok)
    P.wait_all("sp", [("dma:" + k, v) for k, v in P.dma_cnt.items() if k.startswith("sp_")])

    P.finalize(sems)

    @block.sync
    def _(e):
        P.emit("sp", e)

    @block.gpsimd
    def _(e):
        P.emit("pool", e)

    @block.tensor
    def _(e):
        P.emit("pe", e)

    @block.scalar
    def _(e):
        P.emit("act", e)

    @block.vector
    def _(e):
        P.emit("dve", e)

    es.close()
    return nc, dbg_list


def host_layout(inp, nseq_total):
    f = lambda a: np.ascontiguousarray(np.asarray(a, dtype=np.float32))
    out = {}
    x = f(inp["x"])
    out["x_fm"] = np.ascontiguousarray(x.transpose(0, 2, 1).reshape(x.shape[0], KC, 128, S).transpose(0, 2, 1, 3))
    out["pos"] = np.ascontiguousarray(np.asarray(inp["positions"], dtype=np.int32))
    g = np.zeros((128, 128), np.float32)
    vecs = [f(inp["ffn1_norm"])[l] for l in range(4)] + [f(inp["mix_norm"])[l] for l in range(4)] + \
           [f(inp["ffn2_norm"])[l] for l in range(4)] + [f(inp["kv_norm"])]
    for i, v in enumerate(vecs):
        g[:, i * 8:(i + 1) * 8] = v.reshape(8, 128).T
    g[:, 104:106] = f(inp["mla_kv_a_norm"]).reshape(2, 128).T
    for j in range(2):
        g[:, 106 + 3 * j:109 + 3 * j] = f(inp["mla_q_norm"])[j].reshape(3, 128).T
    g[:, 112:114] = f(inp["gdn_out_norm"]).T
    g[:, 120:128] = f(inp["final_norm"]).reshape(8, 128).T
    out["gains"] = g
    cw = f(inp["gdn_conv_w"])
    out["convw"] = np.ascontiguousarray(cw.reshape(2, 4, 24, 128).transpose(3, 0, 1, 2).reshape(128, 192))
    rows = np.zeros((2, 2, 128), np.float32)
    for i in range(2):
        rows[i, 0] = np.repeat(f(inp["gdn_a_log"])[i], 16)
        rows[i, 1] = np.repeat(f(inp["gdn_dt_bias"])[i], 16)
    out["rows"] = rows
    wffn = np.empty((2, 4, NJG, 128, 6144), np.float32)
    for fi, (gu, dn) in enumerate([("ffn1_w_gu", "ffn1_w_down"), ("ffn2_w_gu", "ffn2_w_down")]):
        wgu = f(inp[gu])
        wd = f(inp[dn])
        a = wgu.reshape(4, 8, 128, 2, NJG, 256)
        wffn[fi, :, :, :, :4096] = a.transpose(0, 4, 2, 1, 3, 5).reshape(4, NJG, 128, 4096)
        b = wd.reshape(4, NJG, 2, 128, 1024)
        wffn[fi, :, :, :, 4096:] = b.transpose(0, 1, 3, 2, 4).reshape(4, NJG, 128, 2048)
    out["wffn"] = wffn
    win = f(inp["gdn_w_in"])
    a = win[:, :, :4096].reshape(2, 8, 128, 4, 8, 128)
    out["win"] = np.ascontiguousarray(a.transpose(0, 4, 2, 1, 3, 5).reshape(2, 8, 128, 4096))
    out["wab"] = np.ascontiguousarray(win[:, :, 4096:].reshape(2, 8, 128, 16).transpose(0, 2, 1, 3).reshape(2, 128, 128))
    out["wout"] = np.ascontiguousarray(f(inp["gdn_w_out"]).reshape(2, 8, 128, 1024))
    perm = np.concatenate([np.arange(32, 64), np.arange(0, 32)])
    wkva = f(inp["mla_w_kv_a"])
    wk = np.concatenate([wkva, wkva[:, 256:320][:, perm]], axis=1)
    out["wkva"] = np.ascontiguousarray(wk.reshape(8, 128, 384).transpose(1, 0, 2).reshape(128, 8 * 384))
    out["wkvb"] = np.ascontiguousarray(f(inp["mla_w_kv_b"]).reshape(2, 128, 2048).transpose(1, 0, 2).reshape(128, 4096))
    out["wdq"] = np.ascontiguousarray(f(inp["mla_w_dq"]).reshape(2, 8, 128, 384).transpose(0, 2, 1, 3).reshape(2, 128, 3072))
    wuq = f(inp["mla_w_uq"]).reshape(2, 3, 128, 8, 192)
    wuqx = np.concatenate([wuq, wuq[..., 128:192][..., perm]], axis=-1)
    out["wuq"] = np.ascontiguousarray(wuqx.transpose(0, 2, 1, 3, 4).reshape(2, 128, 3 * 8 * 256))
    out["wo"] = np.ascontiguousarray(f(inp["mla_w_o"]).reshape(2, 8, 128, 1024).transpose(0, 2, 1, 3).reshape(2, 128, 8192))
    out["consts"] = CONSTS
    return out


def run(inputs, cfg, ncores=NCORES, batches=None):
    hl = host_layout(inputs, None)
    nc, dbg_list = build(cfg)
    B = hl["x_fm"].shape[0]
    if batches is None:
        batches = [list(range(c * cfg.nseq, (c + 1) * cfg.nseq)) for c in range(ncores)]
    in_maps = []
    shared = {k: v for k, v in hl.items() if k not in ("x_fm", "pos")}
    for bl in batches:
        m = dict(shared)
        m["x_fm"] = np.ascontiguousarray(hl["x_fm"][bl])
        m["pos"] = np.ascontiguousarray(hl["pos"][bl])
        in_maps.append(m)
    res = run_bass_kernel_spmd(nc, in_maps, core_ids=list(range(len(batches))))
    ys = []
    for r in res.results:
        y = r["y_fm"]
        ys.append(y.transpose(0, 3, 2, 1).reshape(y.shape[0], S, D))
    return np.concatenate(ys, axis=0), res, dbg_list


def kernel(**inputs):
    y, _, _ = run(inputs, Cfg(nseq=2))
    return np.ascontiguousarray(y.astype(np.float32))
```

```python
import numpy as np
import concourse.bass as bass
import concourse.mybir as mybir
from concourse.bass_utils import run_bass_kernel_spmd

F32 = mybir.dt.float32
BF16 = mybir.dt.bfloat16
I32 = mybir.dt.int32
AF = mybir.ActivationFunctionType
ALU = mybir.AluOpType

D = 1024
S = 2048
KC = 8
DFF = 2816
NJG = 11
NT512 = 4
EPS = 1e-6
BIG = 30000.0
NCORES = 8
TWO_PI = 2.0 * np.pi
C1_2PI = 6.28125
C2_2PI = float(TWO_PI - 6.28125)

CI = {}


def _build_consts():
    cols = []

    def add(name, arr):
        CI[name] = sum(a.shape[1] for a in cols)
        cols.append(arr.astype(np.float32))

    p = np.arange(128)[:, None]
    f = np.arange(128)[None, :]
    add("ident", (p == f))
    add("nident", -(p == f).astype(np.float32))
    add("ones", np.ones((128, 128)))
    add("tri", (p <= f))
    add("pos_ls", np.where(p > f, 0.0, BIG))
    add("neg_us", np.where(f > p, 0.0, -BIG))
    add("neg_ui", np.where(f >= p, 0.0, -BIG))
    bd = lambda b: ((p // b) == (f // b)).astype(np.float32)
    add("bd16", bd(16))
    add("m32", bd(32) - bd(16))
    add("m64", bd(64) - bd(32))
    add("m128", 1.0 - bd(64))
    half = 32
    inv = (10000.0 ** (-np.arange(half, dtype=np.float32) / half)).astype(np.float32)
    invp = np.zeros((128, 1), np.float32)
    invp[:64, 0] = np.concatenate([inv, inv])
    sgn = np.zeros((128, 1), np.float32)
    sgn[:32] = -1.0
    sgn[32:64] = 1.0
    add("inv", invp)
    add("sgn", sgn)
    add("eps", np.full((128, 1), EPS))
    add("one", np.full((128, 1), 1.0))
    return np.concatenate(cols, axis=1)


CONSTS = _build_consts()
NCONST = CONSTS.shape[1]


class V:
    __slots__ = ("ap", "keys", "meta")

    def __init__(self, ap, keys):
        self.ap = ap
        self.keys = tuple(keys)
        self.meta = None

    def __getitem__(self, idx):
        return V(self.ap[idx], self.keys)


class Op:
    __slots__ = ("fn", "deps", "inc", "semval", "waits", "real")

    def __init__(self, fn, deps, real=True):
        self.fn = fn
        self.deps = deps
        self.inc = False
        self.semval = 0
        self.waits = []
        self.real = real


ENGS = ("pe", "act", "dve", "pool", "sp")
SAME_ENGINE_SYNC = True
STRICT_SAME = True
NDSEM = 16
GDN_OFFSET = 13
MLA_OFFSET = 12


class Prog:
    def __init__(self, nc):
        self.nc = nc
        self.q = {e: [] for e in ENGS}
        self.wr = {}
        self.rd = {}
        self.dma_cnt = {}
        self.dma_rr = {}
        self.last_real = {}

    def reset_tracking(self):
        self.wr = {}
        self.rd = {}

    def _deps(self, reads, writes):
        deps = {}
        for r in reads:
            if r in self.wr:
                deps[self.wr[r]] = True
            if isinstance(r, tuple) and r[0] == "ps":
                for tok in self.rd.get(r, {}).values():
                    deps.setdefault(tok, False)
        for w in writes:
            if w in self.wr:
                deps[self.wr[w]] = deps.get(self.wr[w], False) or STRICT_SAME
            for tok in self.rd.get(w, {}).values():
                deps[tok] = deps.get(tok, False) or STRICT_SAME
        out = {}
        for (k, i), st in deps.items():
            t = (k, self.dma_cnt[k[4:]]) if k.startswith("dma:") else (k, i)
            out[t] = out.get(t, False) or st
        return out

    def _commit(self, tok, reads, writes, rkey):
        for w in writes:
            self.wr[w] = tok
            self.rd[w] = {}
        for r in reads:
            self.rd.setdefault(r, {})[rkey] = tok

    def op(self, eng, fn, reads=(), writes=()):
        reads = [k for v in reads for k in (v.keys if isinstance(v, V) else (v,))]
        writes = [k for v in writes for k in (v.keys if isinstance(v, V) else (v,))]
        deps = self._deps(reads, writes)
        o = Op(fn, deps)
        self.q[eng].append(o)
        tok = (eng, len(self.q[eng]) - 1)
        self.last_real[eng] = tok
        self._commit(tok, reads, writes, eng)
        return tok

    def dma(self, eng, sem, fns, reads=(), writes=()):
        reads = [k for v in reads for k in (v.keys if isinstance(v, V) else (v,))]
        writes = [k for v in writes for k in (v.keys if isinstance(v, V) else (v,))]
        deps = self._deps(reads, writes)
        rr = self.dma_rr.get(eng, 0)
        self.dma_rr[eng] = (rr + 1) % NDSEM
        sem = "%s_d%d" % (eng, rr)
        if sem in self.dma_cnt:
            deps[("dma:" + sem, self.dma_cnt[sem])] = True
        self.dma_cnt[sem] = self.dma_cnt.get(sem, 0) + 16 * len(fns)
        val = self.dma_cnt[sem]

        def fn(e, fns=fns, sem=sem):
            for f in fns:
                f(e).then_inc(self.sems[sem], 16)
            return None
        o = Op(fn, deps)
        self.q[eng].append(o)
        tok = ("dma:" + sem, val)
        self._commit(tok, reads, writes, "dma:" + sem)
        return tok

    def barrier(self):
        toks = dict(self.last_real)
        dtoks = [("dma:" + s, v) for s, v in self.dma_cnt.items()]
        for e in ENGS:
            deps = {t: True for k, t in toks.items() if k != e}
            deps.update({t: True for t in dtoks})
            self.q[e].append(Op(None, deps, real=False))
        self.reset_tracking()

    def wait_all(self, eng, toks):
        self.q[eng].append(Op(None, {t: True for t in toks}, real=False))

    def finalize(self, sems):
        self.sems = sems
        for e in ENGS:
            waited = {}
            for o in self.q[e]:
                best = {}
                for (k, idx), strong in o.deps.items():
                    if k == e and (e == "pe" or not strong or not SAME_ENGINE_SYNC):
                        continue
                    best[k] = max(best.get(k, -1), idx)
                for (k, idx) in sorted(best.items()):
                    if idx <= waited.get(k, -1 if not k.startswith("dma:") else 0):
                        continue
                    waited[k] = idx
                    if k.startswith("dma:"):
                        o.waits.append((k[4:], idx, None))
                    else:
                        tgt = self.q[k][idx]
                        assert tgt.real
                        tgt.inc = True
                        o.waits.append((k, None, tgt))
        for e in ENGS:
            c = 0
            for o in self.q[e]:
                if o.inc:
                    c += 1
                    o.semval = c

    def emit(self, eng_name, eng):
        for o in self.q[eng_name]:
            for (k, val, tgt) in o.waits:
                if tgt is None:
                    eng.wait_ge(self.sems[k], val)
                else:
                    eng.wait_ge(self.sems[k], tgt.semval)
            if o.fn is not None:
                ins = o.fn(eng)
                if o.inc:
                    ins.then_inc(self.sems[eng_name], 1)


class Cfg:
    def __init__(self, nseq=2, stop=None, dbg=False, phases=None):
        self.phases = phases
        self.nseq = nseq
        self.stop = stop
        self.dbg = dbg


def build(cfg):
    nc = bass.Bass("TRN2", target_bir_lowering=False)
    P = Prog(nc)
    NSEQ = cfg.nseq

    def din(name, shape, dt=F32):
        return nc.dram_tensor(name, list(shape), dt, kind="ExternalInput")

    x_fm = din("x_fm", [NSEQ, 128, KC, S])
    pos_d = din("pos", [NSEQ, S], I32)
    consts_d = din("consts", [128, NCONST])
    gains_d = din("gains", [128, 128])
    convw_d = din("convw", [128, 192])
    rows_d = din("rows", [2, 2, 128])
    wffn_d = din("wffn", [2, 4, NJG, 128, 6144])
    win_d = din("win", [2, 8, 128, 4096])
    wab_d = din("wab", [2, 128, 128])
    wout_d = din("wout", [2, 8, 128, 1024])
    wkva_d = din("wkva", [128, 8 * 384])
    wkvb_d = din("wkvb", [128, 2 * 2048])
    wdq_d = din("wdq", [2, 128, 8 * 384])
    wuq_d = din("wuq", [2, 128, 3 * 8 * 256])
    wo_d = din("wo", [2, 128, 8 * 1024])
    y_fm = nc.dram_tensor("y_fm", [NSEQ, 128, KC, S], F32, kind="ExternalOutput")
    scr_d = nc.dram_tensor("scr_rows", [3, 128, 128], F32, kind="Internal")
    dbg_out = {}

    from contextlib import ExitStack
    es = ExitStack()
    Hh = es.enter_context(nc.sbuf_tensor("H", [128, KC, S], F32))
    XNh = es.enter_context(nc.sbuf_tensor("XN", [128, KC, S], BF16))
    PERSh = es.enter_context(nc.sbuf_tensor("PERS", [128, 14336], BF16))
    AWh = es.enter_context(nc.sbuf_tensor("AW", [128, 36864], BF16))
    CONh = es.enter_context(nc.sbuf_tensor("CON", [128, NCONST], F32))
    GAINh = es.enter_context(nc.sbuf_tensor("GAIN", [128, 128], F32))
    CONVh = es.enter_context(nc.sbuf_tensor("CONVW", [128, 192], F32))
    CBh = es.enter_context(nc.sbuf_tensor("CONB", [128, 12 * 128], BF16))
    PSh = [es.enter_context(nc.psum_tensor("ps%d" % i, [128, 512], F32)) for i in range(8)]
    semnames = list(ENGS) + ["%s_d%d" % (e, i) for e in ("sp", "pool") for i in range(NDSEM)]
    sems = {n: es.enter_context(nc.semaphore(n)) for n in semnames}
    block = es.enter_context(nc.Block())

    AWf = AWh.bitcast(F32)
    PERSf = PERSh.bitcast(F32)

    def aw(off, n, key, shape=None, f32=False):
        if f32:
            ap = AWf[:, off // 2:(off + n) // 2]
        else:
            ap = AWh[:, off:off + n]
        if shape is not None:
            names = " ".join("a%d" % i for i in range(len(shape)))
            kw = {"a%d" % i: s for i, s in enumerate(shape[:-1])}
            ap = ap.rearrange("p (%s) -> p %s" % (names, names), **kw)
        return V(ap, [("aw", i) for i in range(off // 512, (off + n - 1) // 512 + 1)])

    def pers(off, n, key, shape=None, f32=False):
        if f32:
            ap = PERSf[:, off // 2:(off + n) // 2]
        else:
            ap = PERSh[:, off:off + n]
        if shape is not None:
            names = " ".join("a%d" % i for i in range(len(shape)))
            kw = {"a%d" % i: s for i, s in enumerate(shape[:-1])}
            ap = ap.rearrange("p (%s) -> p %s" % (names, names), **kw)
        return V(ap, [("pers", i) for i in range(off // 256, (off + n - 1) // 256 + 1)])

    def lo64(v):
        return V(v.ap[0:64], v.keys)

    def cf(name, n=128):
        return V(CONh[:, CI[name]:CI[name] + n], ["const"])

    CB_NAMES = ["ident", "ones", "bd16", "m32", "m64", "m128"]

    def cb(name):
        i = CB_NAMES.index(name)
        return V(CBh[:, i * 128:(i + 1) * 128], ["constb"])

    def cb4(name):
        i = CB_NAMES.index(name)
        return V(bass.AP(CBh, i * 128, [[12 * 128, 128], [0, 4], [1, 128]]), ["constb"])

    def Ht(k, t):
        return V(Hh[:, k, t * 512:(t + 1) * 512], [("H", k, t)])

    def XNt(k, t):
        return V(XNh[:, k, t * 512:(t + 1) * 512], [("XN", k, t)])

    def PS(b, n=512, dt=F32):
        if dt == F32:
            return V(PSh[b][:, 0:n], [("ps", b)])
        return V(PSh[b].bitcast(BF16)[:, 0:n], [("ps", b)])

    def gain(i, k):
        return V(GAINh[:, i * 8 + k:i * 8 + k + 1], ["gain"])

    def gaincol(c):
        return V(GAINh[:, c:c + 1], ["gain"])

    def mm(out, lhsT, rhs, start=True, stop=True):
        return P.op("pe", lambda e: e.matmul(out.ap, lhsT=lhsT.ap, rhs=rhs.ap, start=start, stop=stop),
                    reads=[lhsT, rhs], writes=[out])

    def tr(out, in_, ident):
        return P.op("pe", lambda e: e.transpose(out.ap, in_.ap, ident.ap), reads=[in_, ident], writes=[out])

    def act(out, in_, func, scale=1.0, bias=None, eng="act"):
        rd = [in_]
        kw = {}
        if isinstance(scale, V):
            rd.append(scale)
            kw["scale"] = scale.ap
        else:
            kw["scale"] = float(scale)
        if isinstance(bias, V):
            rd.append(bias)
            kw["bias"] = bias.ap
        elif bias is not None:
            kw["bias"] = float(bias)
        return P.op(eng, lambda e: e.activation(out=out.ap, in_=in_.ap, func=func, **kw), reads=rd, writes=[out])

    def ts(out, in0, s1, op0, s2=None, op1=None, eng="dve"):
        rd = [in0]
        a1 = s1.ap if isinstance(s1, V) else float(s1)
        if isinstance(s1, V):
            rd.append(s1)
        a2 = None
        if s2 is not None:
            a2 = s2.ap if isinstance(s2, V) else float(s2)
            if isinstance(s2, V):
                rd.append(s2)
        if op1 is None:
            return P.op(eng, lambda e: e.tensor_scalar(out=out.ap, in0=in0.ap, scalar1=a1, scalar2=None, op0=op0),
                        reads=rd, writes=[out])
        return P.op(eng, lambda e: e.tensor_scalar(out=out.ap, in0=in0.ap, scalar1=a1, scalar2=a2, op0=op0, op1=op1),
                    reads=rd, writes=[out])

    def stt(out, in0, sc, in1, op0, op1):
        rd = [in0, in1]
        a = sc.ap if isinstance(sc, V) else float(sc)
        if isinstance(sc, V):
            rd.append(sc)
        return P.op("dve", lambda e: e.scalar_tensor_tensor(out=out.ap, in0=in0.ap, scalar=a, in1=in1.ap,
                                                              op0=op0, op1=op1), reads=rd, writes=[out])

    def tt(out, in0, in1, op, eng="dve"):
        return P.op(eng, lambda e: e.tensor_tensor(out=out.ap, in0=in0.ap, in1=in1.ap, op=op),
                    reads=[in0, in1], writes=[out])

    def cp(out, in_, eng="dve"):
        if eng == "act":
            return act(out, in_, AF.Copy)
        return P.op(eng, lambda e: e.tensor_copy(out=out.ap, in_=in_.ap), reads=[in_], writes=[out])

    def memset(out, val, eng="dve"):
        return P.op(eng, lambda e: e.memset(out.ap, val), writes=[out])

    def load(sem, out, src_ap, eng="pool"):
        return P.dma(eng, sem, [lambda e: e.dma_start(out=out.ap, in_=src_ap)], writes=[out])

    dbg_list = []

    def dbg(name, v, shape, dt=F32):
        if not cfg.dbg:
            return
        t = nc.dram_tensor("dbg_" + name, list(shape), dt, kind="ExternalOutput")
        dbg_list.append("dbg_" + name)
        P.dma("sp", "dbg", [lambda e: e.dma_start(out=t.ap(), in_=v.ap)], reads=[v], writes=["dbgout"])

    P.dma("sp", "ld_const", [lambda e: e.dma_start(out=CONh[:], in_=consts_d.ap()),
                             lambda e: e.dma_start(out=GAINh[:], in_=gains_d.ap()),
                             lambda e: e.dma_start(out=CONVh[:], in_=convw_d.ap())],
          writes=["const", "gain", "convw"])
    for i, nme in enumerate(CB_NAMES):
        cp(V(CBh[:, i * 128:(i + 1) * 128], ["constb"]), cf(nme))
    P.barrier()

    def norm_tile(src_fn, dst_fn, sq_fn, nk, gain_fn, rs, lnt, dim, psb, post_scale=1.0, eps=True):
        for k in range(nk):
            act(sq_fn(k), src_fn(k), AF.Square)
        ps = PS(psb)
        for k in range(nk):
            mm(ps, cb("ones"), sq_fn(k), start=(k == 0), stop=(k == nk - 1))
        act(lnt, ps, AF.Ln, scale=1.0 / dim, bias=cf("eps", 1))
        act(rs, lnt, AF.Exp, scale=-0.5)
        for k in range(nk):
            g = gain_fn(k) if gain_fn is not None else post_scale
            stt(dst_fn(k), src_fn(k), g, rs, ALU.mult, ALU.mult)

    RS = aw(36864 - 2048, 1024, "rs", f32=True)
    LNT = aw(36864 - 1024, 1024, "lnt", f32=True)

    def norm_H_tile(gi, t, psb=6):
        norm_tile(lambda k: Ht(k, t), lambda k: XNt(k, t), lambda k: XNt(k, t), KC,
                  lambda k: gain(gi, k), RS, LNT, D, psb)

    def norm_H_to_XN(gi):
        for t in range(NT512):
            norm_H_tile(gi, t)

    def ffn_phase(f, l, gi, prenormed=False, on_final=None):
        if not prenormed:
            norm_H_to_XN(gi)
        NSLOT = 3
        WS = [aw(i * 6144, 6144, ("ws", i)) for i in range(NSLOT)]
        ACTB = [aw(18432 + i * 1024, 1024, ("actb", i), shape=[2, 512]) for i in range(2)]
        SG = [aw(20480 + i * 1024, 1024, ("sg", i), f32=True) for i in range(2)]

        def issue_load(jg):
            load("w%d" % (jg % NSLOT), WS[jg % NSLOT], wffn_d.ap()[f, l, jg])

        for jg in range(min(NSLOT, NJG)):
            issue_load(jg)
        cnt = [0]

        def down(jg, t, ab):
            w = WS[jg % NSLOT]
            for m in range(KC):
                ps = PS(4 + (m % 4))
                for j in range(2):
                    mm(ps, w[:, 4096 + j * 1024 + m * 128:4096 + j * 1024 + (m + 1) * 128], V(ab.ap[:, j, :], [ab.keys[j]]),
                       start=(j == 0), stop=(j == 1))
                stt(Ht(m, t), ps, 0.5, Ht(m, t), ALU.mult, ALU.add)
            if jg == NJG - 1 and on_final is not None:
                on_final(t)

        pending = None
        for jg in range(NJG):
            w = WS[jg % NSLOT]
            for t in range(NT512):
                ab = ACTB[cnt[0] % 2]
                cnt[0] += 1
                for j in range(2):
                    pg = PS(j)
                    pu = PS(2 + j)
                    for k in range(KC):
                        mm(pg, w[:, k * 512 + j * 128:k * 512 + (j + 1) * 128], XNt(k, t), start=(k == 0), stop=(k == KC - 1))
                    for k in range(KC):
                        mm(pu, w[:, k * 512 + 256 + j * 128:k * 512 + 256 + (j + 1) * 128], XNt(k, t),
                           start=(k == 0), stop=(k == KC - 1))
                    act(SG[j], pg, AF.Silu)
                    tt(V(ab.ap[:, j, :], [ab.keys[j]]), SG[j], pu, ALU.mult)
                if pending is not None:
                    down(*pending)
                pending = (jg, t, ab)
            if jg + NSLOT < NJG:
                down(*pending)
                pending = None
                issue_load(jg + NSLOT)
        if pending is not None:
            down(*pending)

    CKV = pers(0, 4096, "ckv", shape=[2, S])
    KR = pers(4096, 2048, "kr")
    CC = pers(6144, 4096, "cc", f32=True)
    SS = pers(10240, 4096, "ss", f32=True)

    def rope_tables(s):
        PI32 = V(AWh.bitcast(I32)[0:64, 0:2048], aw(0, 4096, None).keys)
        ANG = lo64(aw(4096, 4096, None, f32=True))
        T1 = lo64(aw(8192, 4096, None, f32=True))
        T2 = lo64(aw(12288, 4096, None, f32=True))
        KI = V(AWh.bitcast(I32)[0:64, 8192:10240], aw(16384, 4096, None).keys)
        P.dma("sp", "ld_x", [lambda e: e.dma_start(out=PI32.ap, in_=bass.AP(pos_d, s * S, [[0, 64], [1, S]]))],
              writes=[PI32])
        cp(ANG, PI32)
        ts(ANG, ANG, V(CONh[0:64, CI["inv"]:CI["inv"] + 1], ["const"]), ALU.mult)
        ts(T1, ANG, 1.0 / TWO_PI, ALU.mult)
        cp(KI, T1)
        cp(T1, KI)
        stt(T2, T1, -C1_2PI, ANG, ALU.mult, ALU.add)
        stt(T2, T1, -C2_2PI, T2, ALU.mult, ALU.add)
        PIC = 3.1415925
        ts(T1, T2, -PIC, ALU.max, PIC, ALU.min)
        sg = V(CONh[0:64, CI["sgn"]:CI["sgn"] + 1], ["const"])
        act(V(SS.ap[0:64], SS.keys), T1, AF.Sin, scale=sg)
        ts(T1, T2, float(np.pi / 2), ALU.is_gt, -TWO_PI, ALU.mult)
        stt(T1, T2, float(np.pi / 2), T1, ALU.add, ALU.add)
        ts(T1, T1, -PIC, ALU.max, PIC, ALU.min)
        act(V(CC.ap[0:64], CC.keys), T1, AF.Sin)

    def shared_kv_phase(s, prenormed=False, on_final=None):
        rope_tables(s)
        if not prenormed:
            norm_H_to_XN(12)
        WK = aw(0, 3072, "wkva", shape=[8, 384])
        RAWC = aw(4096, 2048, "rawc", shape=[2, 512], f32=True)
        SQC = aw(6144, 1024, "sqc", shape=[2, 512])
        T1 = lo64(aw(8192, 1024, None, f32=True))
        T2 = lo64(aw(9216, 1024, None, f32=True))
        load("wm", WK, wkva_d.ap())
        for t in range(NT512):
            tsl = slice(t * 512, (t + 1) * 512)
            for c in range(2):
                ps = PS(c)
                for k in range(KC):
                    mm(ps, WK[:, k, c * 128:(c + 1) * 128], XNt(k, t), start=(k == 0), stop=(k == KC - 1))
                cp(RAWC[:, c, :], ps, eng="act")
            norm_tile(lambda c: RAWC[:, c, :], lambda c: CKV[:, c, tsl], lambda c: SQC[:, c, :], 2,
                      lambda c: gaincol(104 + c), RS, LNT, 256, 6)
            px = V(PSh[2][0:64, :], [("ps", 2)])
            pxp = V(PSh[3][0:64, :], [("ps", 3)])
            for k in range(KC):
                mm(px, WK[:, k, 256:320], XNt(k, t), start=(k == 0), stop=(k == KC - 1))
            for k in range(KC):
                mm(pxp, WK[:, k, 320:384], XNt(k, t), start=(k == 0), stop=(k == KC - 1))
            tt(T1, px, V(CC.ap[0:64, tsl], CC.keys), ALU.mult)
            tt(T2, pxp, V(SS.ap[0:64, tsl], SS.keys), ALU.mult)
            tt(V(KR.ap[0:64, tsl], KR.keys), T1, T2, ALU.add)
            if on_final is not None:
                on_final(t)

    def mla_phase(l, prenormed=False, on_final=None):
        j = l - 2
        if not prenormed:
            norm_H_to_XN(4 + l)
        WUQ = aw(0, 6144, "wuq", shape=[3, 8, 256])
        WKB = aw(6144, 4096, "wkvb", shape=[2, 8, 256])
        CQ = aw(10240, 6144, "cq", shape=[3, S])
        WO = aw(0, 8192, None, shape=[8, 1024])
        T0 = 16384
        WDQ = aw(T0, 3072, "wdq", shape=[8, 384])
        CQR = aw(T0 + 3072, 3072, "cqr", shape=[3, 512], f32=True)
        SQQ = aw(T0 + 6144, 1536, "sqq", shape=[3, 512])
        load("wm", WDQ, wdq_d.ap()[j])
        load("w0", WUQ, wuq_d.ap()[j])
        load("w0", WKB, wkvb_d.ap())
        for t in range(NT512):
            tsl = slice(t * 512, (t + 1) * 512)
            for c in range(3):
                ps = PS(c)
                for k in range(KC):
                    mm(ps, WDQ[:, k, c * 128:(c + 1) * 128], XNt(k, t), start=(k == 0), stop=(k == KC - 1))
                cp(CQR[:, c, :], ps, eng="act")
            norm_tile(lambda c: CQR[:, c, :], lambda c: CQ[:, c, tsl], lambda c: SQQ[:, c, :], 3,
                      lambda c: gaincol(106 + j * 3 + c), RS, LNT, 384, 6)
        AO = lambda h, t: V(XNh[:, h, t * 512:(t + 1) * 512], [("XN", h, t)])
        scale = float((128 + 64) ** -0.5)
        NS = 2
        pools = [SlotPool(32 + si * 18, 18, "ms%d" % si) for si in range(NS)]
        remaining = [32]

        def stream(si):
            pool = pools[si]
            KTh = pool.alloc(2048)
            Vh = pool.alloc(2048, shape=[16, 128])
            PT = [pool.alloc(512) for _ in range(3)]
            ptc = 0
            for h in range(si, 8, NS):
                for t in range(NT512):
                    b = yield from acq()
                    ps = PS(b)
                    for c in range(2):
                        mm(ps, WKB[:, c, h, 0:128], CKV[:, c, t * 512:(t + 1) * 512], start=(c == 0), stop=(c == 1))
                    yield
                    cp(KTh[:, t * 512:(t + 1) * 512], ps, eng="act")
                    rel(b)
                for t4 in range(4):
                    b = yield from acq()
                    ps = PS(b)
                    for tq in range(4):
                        tk = t4 * 4 + tq
                        for c in range(2):
                            mm(ps[:, tq * 128:(tq + 1) * 128], CKV[:, c, tk * 128:(tk + 1) * 128], WKB[:, c, h, 128:256],
                               start=(c == 0), stop=(c == 1))
                    yield
                    cp(V(Vh.ap[:, t4 * 4:(t4 + 1) * 4, :].rearrange("p a b -> p (a b)"), Vh.keys), ps, eng="dve")
                    rel(b)
                for a in range(4):
                    gsl = slice(a * 512, (a + 1) * 512)
                    QN = pool.alloc(512)
                    QR = pool.alloc(512)
                    RT1 = pool.alloc(1024, f32=True)
                    RT2 = pool.alloc(1024, f32=True)
                    b = yield from acq()
                    ps = PS(b)
                    for c in range(3):
                        mm(ps, WUQ[:, c, h, 0:128], CQ[:, c, gsl], start=(c == 0), stop=(c == 2))
                    yield
                    cp(QN, ps, eng="act")
                    rel(b)
                    bx = yield from acq()
                    bxp = yield from acq()
                    px = V(PSh[bx][0:64, :], [("ps", bx)])
                    pxp = V(PSh[bxp][0:64, :], [("ps", bxp)])
                    for c in range(3):
                        mm(px, WUQ[:, c, h, 128:192], CQ[:, c, gsl], start=(c == 0), stop=(c == 2))
                    for c in range(3):
                        mm(pxp, WUQ[:, c, h, 192:256], CQ[:, c, gsl], start=(c == 0), stop=(c == 2))
                    remaining[0] -= 1
                    if remaining[0] == 0:
                        load("w1", WO, wo_d.ap()[j])
                    yield
                    r1 = V(RT1.ap[0:64], RT1.keys)
                    r2 = V(RT2.ap[0:64], RT2.keys)
                    tt(r1, px, V(CC.ap[0:64, gsl], CC.keys), ALU.mult)
                    rel(bx)
                    tt(r2, pxp, V(SS.ap[0:64, gsl], SS.keys), ALU.mult)
                    rel(bxp)
                    yield
                    tt(V(QR.ap[0:64], QR.keys), r1, r2, ALU.add)
                    pool.free(RT1, RT2)
                    ACC = pool.alloc(1024, f32=True)
                    bo = yield from acq()
                    pso = PS(bo)
                    nk = 4 * a + 4

                    def qk(jt):
                        c0 = max(0, jt - 4 * a) * 128
                        bst = yield from acq()
                        pst = PS(bst)
                        mm(pst[:, c0:], KTh[:, jt * 128:(jt + 1) * 128], QN[:, c0:], start=True, stop=False)
                        mm(pst[:, c0:], V(KR.ap[0:64, jt * 128:(jt + 1) * 128], KR.keys), V(QR.ap[0:64, c0:], QR.keys),
                           start=False, stop=True)
                        return bst, pst, c0
                    cur = yield from qk(0)
                    for jt in range(nk):
                        nxt_ = None
                        if jt + 1 < nk:
                            nxt_ = yield from qk(jt + 1)
                        bst, pst, c0 = cur
                        pt = PT[ptc % 3]
                        ptc += 1
                        yield
                        act(pt[:, c0:], pst[:, c0:], AF.Exp, scale=scale)
                        rel(bst)
                        if jt >= 4 * a:
                            memset(V(pt.ap[64:128, c0:c0 + 64], pt.keys), 0.0)
                        yield
                        mm(pso[:, c0:], Vh[:, jt, :], pt[:, c0:], start=(jt == 0), stop=(jt == nk - 1))
                        if jt == 0:
                            cp(ACC, pt)
                        else:
                            tt(ACC[:, c0:], ACC[:, c0:], pt[:, c0:], ALU.add)
                        cur = nxt_
                    pool.free(QN, QR)
                    LNS = pool.alloc(1024, f32=True)
                    RINV = pool.alloc(1024, f32=True)
                    bs = yield from acq()
                    pss = PS(bs)
                    mm(pss, cf("ones"), ACC)
                    yield
                    act(LNS, pss, AF.Ln)
                    rel(bs)
                    act(RINV, LNS, AF.Exp, scale=-1.0)
                    yield
                    tt(AO(h, a), pso, RINV, ALU.mult)
                    rel(bo)
                    pool.free(LNS, RINV, ACC)

        run_streams([stream(si) for si in range(NS)], offset=MLA_OFFSET)
        assert len(free_banks) == 8, free_banks
        for t in range(NT512):
            for m in range(KC):
                ps = PS(m % 2)
                for h in range(8):
                    mm(ps, WO[:, h, m * 128:(m + 1) * 128], AO(h, t), start=(h == 0), stop=(h == 7))
                tt(Ht(m, t), ps, Ht(m, t), ALU.add)
            if on_final is not None:
                on_final(t)

    class SlotPool:
        def __init__(self, base_slot, nslots, name):
            self.base = base_slot
            self.n = nslots
            self.name = name
            self.used = [False] * nslots
            self.peak = 0

        def alloc(self, nel, shape=None, f32=False):
            ns = (nel + 511) // 512
            for st in range(self.n - ns + 1):
                if not any(self.used[st:st + ns]):
                    for i in range(st, st + ns):
                        self.used[i] = True
                    self.peak = max(self.peak, sum(self.used))
                    v = aw((self.base + st) * 512, nel, None, shape=shape, f32=f32)
                    v.meta = (st, ns)
                    return v
            raise RuntimeError("slot pool %s exhausted (%d/%d used, need %d)" % (self.name, sum(self.used), self.n, ns))

        def free(self, *vs):
            for v in vs:
                st, ns = v.meta
                for i in range(st, st + ns):
                    assert self.used[i]
                    self.used[i] = False

    free_banks = list(range(8))

    def acq():
        while not free_banks:
            yield
        return free_banks.pop(0)

    def rel(*bs):
        for b in bs:
            free_banks.append(b)

    def run_streams(gens, offset=0):
        gens = list(gens)
        delay = {id(g): i * offset for i, g in enumerate(gens)}
        while gens:
            for g in list(gens):
                if delay[id(g)] > 0:
                    delay[id(g)] -= 1
                    continue
                try:
                    next(g)
                except StopIteration:
                    gens.remove(g)

    def norm_gen(src, dst, sq, rs, lnt, dim, gainv=None, post_scale=1.0):
        act(sq, src, AF.Square)
        b = yield from acq()
        ps = PS(b)
        mm(ps, cb("ones"), sq)
        yield
        act(lnt, ps, AF.Ln, scale=1.0 / dim, bias=cf("eps", 1))
        rel(b)
        act(rs, lnt, AF.Exp, scale=-0.5)
        yield
        stt(dst, src, gainv if gainv is not None else post_scale, rs, ALU.mult, ALU.mult)

    def gdn_phase(l, s, prenormed=False, on_final=None):
        if not prenormed:
            norm_H_to_XN(4 + l)
        li = l
        done_cnt = [0] * 4
        po = [0]

        def PA(key):
            v = pers(po[0], 256, key, f32=True)
            po[0] += 256
            return v
        G_, GC, LSB, SBv, NSB, EGC, EKD, EGL, C1, C2, KBS, TMPa, TMPb, ROWA, ROWD = [PA("sc%d" % i) for i in range(15)]
        WIN = [pers(3840 + i * 4096, 4096, ("win", i), shape=[8, 4, 128]) for i in range(2)]
        WAB = pers(3840 + 8192, 128, "wab", shape=[8, 16])
        load("wm", WAB, wab_d.ap()[li])
        P.dma("sp", "ld_x", [lambda e: e.dma_start(out=ROWA.ap, in_=bass.AP(rows_d, (li * 2 + 0) * 128, [[0, 128], [1, 128]])),
                             lambda e: e.dma_start(out=ROWD.ap, in_=bass.AP(rows_d, (li * 2 + 1) * 128, [[0, 128], [1, 128]]))],
              writes=[ROWA, ROWD])
        psab = PS(0, 256)
        for t in range(16):
            for k in range(KC):
                mm(psab[:, t * 16:(t + 1) * 16], V(XNh[:, k, t * 128:(t + 1) * 128], [("XN", k, t // 4)]), WAB[:, k, :],
                   start=(k == 0), stop=(k == KC - 1))
        ab3 = V(psab.ap.rearrange("p (t c) -> p t c", c=16), psab.keys)
        v3 = lambda v: V(v.ap.rearrange("p (h t) -> p t h", h=8), v.keys)
        tt(v3(TMPa), ab3[:, :, 0:8], v3(ROWD), ALU.add)
        act(TMPa, TMPa, AF.Exp)
        act(TMPa, TMPa, AF.Ln, bias=cf("one", 1))
        act(ROWA, ROWA, AF.Exp)
        stt(G_, TMPa, -1.0, ROWA, ALU.mult, ALU.mult)
        act(v3(TMPb), ab3[:, :, 8:16], AF.Exp, scale=-1.0)
        act(TMPb, TMPb, AF.Ln, bias=cf("one", 1))
        ts(LSB, TMPb, -0.5, ALU.mult)
        act(SBv, TMPb, AF.Exp, scale=-0.5)
        ts(NSB, SBv, -1.0, ALU.mult)
        psc = PS(1, 128)
        mm(psc, cf("tri"), G_)
        cp(GC, psc)
        psl = PS(2, 128)
        mm(psl, cf("ones"), G_)
        act(EGL, psl, AF.Exp)
        tt(TMPa, psl, GC, ALU.subtract)
        act(EKD, TMPa, AF.Exp)
        act(EGC, GC, AF.Exp)
        tt(C1, GC, LSB, ALU.add)
        tt(C2, GC, LSB, ALU.subtract)
        tt(KBS, SBv, EGC, ALU.mult)
        for vi, src in enumerate([C2, C1, GC]):
            pst = PS(3 + vi, 128)
            tr(pst, src, cf("ident"))
            dstv = [TMPa, TMPb, EGC][vi]
            cp(dstv, pst)
            P.dma("sp", None, [lambda e, vi=vi, dstv=dstv: e.dma_start(out=scr_d.ap()[vi], in_=dstv.ap)],
                  reads=[dstv], writes=[("scr", vi)])

        sc = lambda arr, col: V(arr.ap[:, col:col + 1], arr.keys)
        cw = lambda tap, chunk: V(CONVh[:, (li * 4 + tap) * 24 + chunk:(li * 4 + tap) * 24 + chunk + 1], ["convw"])
        dk_scale = float(128 ** -0.5)
        ident4 = cb4("ident")
        fl = lambda v: V(v.ap.rearrange("p a b -> p (a b)"), v.keys)
        NS = 2
        pools = [SlotPool(si * 34, 34, "gs%d" % si) for si in range(NS)]

        def stream(si):
            pool = pools[si]
            Wn = WIN[si]
            WOh = pool.alloc(1024)
            RSs = pool.alloc(1024, f32=True)
            LNs = pool.alloc(1024, f32=True)
            MISC = pool.alloc(512)
            HALO = V(MISC.ap[:, 0:16].rearrange("p (a b) -> p a b", a=4), MISC.keys)
            Sb = V(MISC.ap[:, 128:256], MISC.keys)
            VN = V(MISC.ap[:, 256:512].rearrange("p (a b) -> p a b", a=2), MISC.keys)
            DG = pool.alloc(1536, shape=[12, 128])
            Sst = pool.alloc(256, f32=True)
            heads = list(range(si, 8, NS))
            load("w%d" % si, Wn, win_d.ap()[li, heads[0]])
            load("w%d" % si, WOh, wout_d.ap()[li, heads[0]])
            its = [(hi, h, bi) for hi, h in enumerate(heads) for bi in range(4)]

            def stage1(hi, h, bi):
                if bi == 0:
                    for ci in range(3):
                        for tap in range(4):
                            act(DG[:, ci * 4 + tap, :], cb("ident"), AF.Copy, scale=cw(tap, ci * 8 + h))
                RAW = pool.alloc(1024)
                QT = pool.alloc(512)
                KT = pool.alloc(512)
                VTf = pool.alloc(512)
                GT = pool.alloc(512)
                dsts = [QT, KT, VTf]
                for ci in range(3):
                    b = yield from acq()
                    ps = PS(b)
                    for k in range(KC):
                        mm(ps, Wn[:, k, ci, :], XNt(k, bi), start=(k == 0), stop=(k == KC - 1))
                    if bi == 0:
                        memset(RAW[:, 0:3], 0.0)
                    else:
                        cp(RAW[:, 0:3], HALO[:, ci, 0:3])
                    yield
                    cp(RAW[:, 3:515], ps, eng="act")
                    rel(b)
                    yield
                    b2 = yield from acq()
                    pc = PS(b2)
                    for tap in range(4):
                        mm(pc, DG[:, ci * 4 + tap, :], RAW[:, tap:tap + 512], start=(tap == 0), stop=(tap == 3))
                    cp(HALO[:, ci, 0:3], RAW[:, 512:515])
                    yield
                    act(dsts[ci], pc, AF.Silu)
                    rel(b2)
                b = yield from acq()
                ps = PS(b)
                for k in range(KC):
                    mm(ps, Wn[:, k, 3, :], XNt(k, bi), start=(k == 0), stop=(k == KC - 1))
                if bi == 3 and hi + 1 < len(heads):
                    load("w%d" % si, Wn, win_d.ap()[li, heads[hi + 1]])
                yield
                act(GT, ps, AF.Silu)
                rel(b)
                pool.free(RAW)
                SQ = pool.alloc(512)
                yield from norm_gen(QT, QT, SQ, RSs, LNs, 1.0, post_scale=dk_scale)
                yield from norm_gen(KT, KT, SQ, RSs, LNs, 1.0, post_scale=1.0)
                pool.free(SQ)
                return QT, KT, VTf, GT

            state = {}

            def body(hi, h, bi, QT, KT, VTf, GT):
                if True:
                    if bi == 0:
                        memset(Sst, 0.0)
                        memset(Sb, 0.0)
                    X1 = pool.alloc(1024, shape=[4, 128], f32=True)
                    X2 = pool.alloc(1024, shape=[4, 128], f32=True)
                    X3 = pool.alloc(1024, shape=[4, 128], f32=True)
                    EGR = pool.alloc(512, shape=[4, 128])
                    col0 = h * 16 + bi * 4
                    for vi, Xv in enumerate([X1, X2, X3]):
                        P.dma("sp", None, [lambda e, vi=vi, Xv=Xv: e.dma_start(
                            out=fl(Xv).ap, in_=bass.AP(scr_d, vi * 16384 + col0 * 128, [[0, 128], [1, 512]]))],
                            reads=[("scr", vi)], writes=[Xv])
                    VTM = pool.alloc(512, shape=[4, 128])
                    KDEC = pool.alloc(512, shape=[4, 128])
                    KBG = pool.alloc(512, shape=[4, 128])
                    bk = yield from acq()
                    bv = yield from acq()
                    pk = PS(bk, 512, BF16)
                    pv = PS(bv, 512, BF16)
                    for t in range(4):
                        tr(pk[:, t * 128:(t + 1) * 128], KT[:, t * 128:(t + 1) * 128], cb("ident"))
                    for t in range(4):
                        tr(pv[:, t * 128:(t + 1) * 128], VTf[:, t * 128:(t + 1) * 128], cb("ident"))
                    yield
                    for t in range(4):
                        col = h * 16 + bi * 4 + t
                        ts(KDEC[:, t, :], pk[:, t * 128:(t + 1) * 128], sc(EKD, col), ALU.mult)
                        ts(KBG[:, t, :], pk[:, t * 128:(t + 1) * 128], sc(KBS, col), ALU.mult)
                    rel(bk)
                    yield
                    for t in range(4):
                        col = h * 16 + bi * 4 + t
                        ts(VTM[:, t, :], pv[:, t * 128:(t + 1) * 128], sc(SBv, col), ALU.mult)
                    rel(bv)
                    pool.free(VTf)
                    act(EGR, X3, AF.Exp)
                    for t in range(4):
                        col = h * 16 + bi * 4 + t
                        stt(X1[:, t, :], X1[:, t, :], sc(C1, col), cf("pos_ls"), ALU.subtract, ALU.add)
                        stt(X2[:, t, :], X2[:, t, :], sc(C2, col), cf("neg_us"), ALU.subtract, ALU.add)
                        stt(X3[:, t, :], X3[:, t, :], sc(GC, col), cf("neg_ui"), ALU.subtract, ALU.add)
                    DLs = pool.alloc(512, shape=[4, 128])
                    DUs = pool.alloc(512, shape=[4, 128])
                    DUi = pool.alloc(512, shape=[4, 128])
                    yield
                    act(DLs, X1, AF.Exp, scale=-1.0)
                    act(DUs, X2, AF.Exp)
                    act(DUi, X3, AF.Exp)
                    pool.free(X1, X2, X3)
                    bkk = yield from acq()
                    bqk = yield from acq()
                    pkk, pqk = PS(bkk), PS(bqk)
                    for t in range(4):
                        c4 = slice(t * 128, (t + 1) * 128)
                        mm(pkk[:, c4], KT[:, c4], KT[:, c4])
                        mm(pqk[:, c4], KT[:, c4], QT[:, c4])
                    yield
                    AL = pool.alloc(512, shape=[4, 128])
                    AU = pool.alloc(512, shape=[4, 128])
                    ATT = pool.alloc(512, shape=[4, 128])
                    QDT = pool.alloc(512, shape=[4, 128])
                    tt(fl(AL), pkk, fl(DLs), ALU.mult)
                    tt(fl(AU), pkk, fl(DUs), ALU.mult)
                    rel(bkk)
                    tt(fl(ATT), pqk, fl(DUi), ALU.mult)
                    rel(bqk)
                    tt(fl(QDT), QT, fl(EGR), ALU.mult)
                    pool.free(DLs, DUs, DUi, EGR, QT, KT)
                    state["go"] = True
                    yield
                    X0, X0T, R, RTr = [pool.alloc(512, shape=[4, 128]) for _ in range(4)]
                    Pa, PTa = [pool.alloc(512, shape=[4, 128]) for _ in range(2)]
                    Pb, PTb = X0, X0T
                    tt(X0, AL, cb4("bd16"), ALU.mult)
                    tt(X0T, AU, cb4("bd16"), ALU.mult)
                    tt(R, ident4, X0, ALU.subtract)
                    tt(RTr, ident4, X0T, ALU.subtract)
                    yield
                    Pc, PTc = X0, X0T
                    nxt = [(Pa, PTa), (Pb, PTb), (Pa, PTa)]

                    def mm4(ps, lh, rh):
                        for t in range(4):
                            mm(ps[:, t * 128:(t + 1) * 128], lh[:, t, :], rh[:, t, :])
                    for kk in range(3):
                        Pn, PTn = nxt[kk]
                        ba = yield from acq()
                        bb = yield from acq()
                        pa, pb = PS(ba), PS(bb)
                        mm4(pa, PTc, Pc)
                        mm4(pb, Pc, PTc)
                        yield
                        cp(fl(Pn), pa, eng="act")
                        cp(fl(PTn), pb, eng="act")
                        rel(ba, bb)
                        yield
                        ba = yield from acq()
                        bb = yield from acq()
                        pa2, pb2 = PS(ba), PS(bb)
                        mm4(pa2, PTn, R)
                        mm4(pb2, Pn, RTr)
                        yield
                        tt(fl(R), fl(R), pa2, ALU.add)
                        tt(fl(RTr), fl(RTr), pb2, ALU.add)
                        rel(ba, bb)
                        yield
                        Pc, PTc = Pn, PTn
                    pool.free(X0, X0T, Pa, PTa)
                    N_, NTr = R, RTr
                    OFF, OFFT, Z, ZT = [pool.alloc(512, shape=[4, 128]) for _ in range(4)]
                    NTT = pool.alloc(512, shape=[4, 128])
                    for lvl, mk in enumerate(["m32", "m64", "m128"]):
                        last = (lvl == 2)
                        tt(OFF, AL, cb4(mk), ALU.mult)
                        if not last:
                            tt(OFFT, AU, cb4(mk), ALU.mult)
                        yield
                        if not last:
                            bz = yield from acq()
                            pz = PS(bz)
                            mm4(pz, OFFT, N_)
                        bzt = yield from acq()
                        pzt = PS(bzt)
                        mm4(pzt, OFF, NTr)
                        yield
                        if not last:
                            cp(fl(Z), pz, eng="act")
                            rel(bz)
                        cp(fl(ZT), pzt, eng="act")
                        rel(bzt)
                        yield
                        if not last:
                            bw = yield from acq()
                            pw = PS(bw)
                            mm4(pw, NTr, Z)
                        bwt = yield from acq()
                        pwt = PS(bwt)
                        mm4(pwt, N_, ZT)
                        yield
                        if not last:
                            tt(fl(N_), fl(N_), pw, ALU.subtract)
                            rel(bw)
                            tt(fl(NTr), fl(NTr), pwt, ALU.subtract)
                        else:
                            tt(fl(NTT), fl(NTr), pwt, ALU.subtract)
                        rel(bwt)
                        yield
                    pool.free(OFF, OFFT, Z, ZT, R, RTr, AL, AU)
                    US = pool.alloc(1024, shape=[4, 128], f32=True)
                    WT = pool.alloc(512, shape=[4, 128])
                    bu = yield from acq()
                    bw_ = yield from acq()
                    pu, pw_ = PS(bu), PS(bw_)
                    for t in range(4):
                        c4 = slice(t * 128, (t + 1) * 128)
                        mm(pu[:, c4], NTT[:, t, :], VTM[:, t, :])
                        mm(pw_[:, c4], KBG[:, t, :], NTT[:, t, :])
                    yield
                    for t in range(4):
                        col = h * 16 + bi * 4 + t
                        ts(US[:, t, :], pu[:, t * 128:(t + 1) * 128], sc(SBv, col), ALU.mult)
                    rel(bu)
                    cp(fl(WT), pw_, eng="act")
                    rel(bw_)
                    pool.free(NTT, VTM, KBG)
                    yield
                    OT = pool.alloc(1024, f32=True)
                    for t in range(4):
                        col = h * 16 + bi * 4 + t
                        vn = VN[:, t % 2, :]
                        b1 = yield from acq()
                        p1 = PS(b1, 128)
                        mm(p1, WT[:, t, :], Sb)
                        yield
                        stt(vn, p1, sc(NSB, col), US[:, t, :], ALU.mult, ALU.add)
                        rel(b1)
                        yield
                        bo = yield from acq()
                        bd = yield from acq()
                        po_ = PS(bo, 128)
                        mm(po_, Sb, QDT[:, t, :], start=True, stop=False)
                        mm(po_, vn, ATT[:, t, :], start=False, stop=True)
                        pds = PS(bd, 128)
                        mm(pds, KDEC[:, t, :], vn)
                        yield
                        stt(Sst, Sst, sc(EGL, col), pds, ALU.mult, ALU.add)
                        rel(bd)
                        cp(OT[:, t * 128:(t + 1) * 128], po_, eng="act")
                        rel(bo)
                        yield
                        cp(Sb, Sst, eng="act")
                        yield
                    pool.free(US, WT, ATT, QDT, KDEC)
                    SQ = pool.alloc(512)
                    OG = pool.alloc(1024, f32=True)
                    OGb = pool.alloc(512)
                    yield from norm_gen(OT, OG, SQ, RSs, LNs, 128.0, gainv=gaincol(112 + li))
                    yield
                    tt(OGb, OG, GT, ALU.mult)
                    pool.free(SQ, OG, OT, GT)
                    yield
                    for m in range(KC):
                        b = yield from acq()
                        ps = PS(b)
                        mm(ps, WOh[:, m * 128:(m + 1) * 128], OGb)
                        yield
                        tt(Ht(m, bi), ps, Ht(m, bi), ALU.add)
                        rel(b)
                    pool.free(OGb)
                    if bi == 3 and hi + 1 < len(heads):
                        load("w%d" % si, WOh, wout_d.ap()[li, heads[hi + 1]])
                    if hi + 1 == len(heads):
                        done_cnt[bi] += 1
                        if done_cnt[bi] == NS and on_final is not None:
                            hb = free_banks.pop(0)
                            on_final(bi, hb)
                            free_banks.append(hb)
                    yield

            cur = yield from stage1(*its[0])
            for ii, it in enumerate(its):
                pre = stage1(*its[ii + 1]) if ii + 1 < len(its) else None
                pre_res = None
                state["go"] = False
                for _ in body(it[0], it[1], it[2], *cur):
                    yield
                    if pre is not None and state["go"]:
                        try:
                            next(pre)
                        except StopIteration as e:
                            pre_res = e.value
                            pre = None
                if pre is not None:
                    pre_res = yield from pre
                cur = pre_res

        run_streams([stream(si) for si in range(NS)], offset=GDN_OFFSET)
        assert len(free_banks) == 8, free_banks

    out_toks = []
    full = cfg.stop is None and cfg.phases is None

    def phase_gi(ph):
        return ph[3] if ph[0] == "ffn" else (12 if ph[0] == "kv" else 4 + ph[1])

    def final_tile(t, psb=6):
        norm_tile(lambda k: Ht(k, t), lambda k: Ht(k, t), lambda k: XNt(k, t), KC,
                  lambda k: gaincol(120 + k), RS, LNT, D, psb)

    for s in range(NSEQ):
        for t in range(NT512):
            P.dma("sp", "ld_x", [lambda e, s=s, k=k, t=t: e.dma_start(out=Hh[:, k, t * 512:(t + 1) * 512],
                                                                    in_=x_fm.ap()[s, :, k, t * 512:(t + 1) * 512])
                                 for k in range(KC)],
                  writes=[("H", k, t) for k in range(KC)])

        def store_tile(t, s=s):
            return P.dma("sp", "st_y", [lambda e, s=s, k=k, t=t: e.dma_start(out=y_fm.ap()[s, :, k, t * 512:(t + 1) * 512],
                                                                           in_=Hh[:, k, t * 512:(t + 1) * 512])
                                        for k in range(KC)],
                         reads=[("H", k, t) for k in range(KC)], writes=[("yout", s, t)])
        phases = []
        for l in range(4):
            phases.append(("ffn", 0, l, l))
            phases.append(("gdn", l) if l < 2 else ("mla", l))
            phases.append(("ffn", 1, l, 8 + l))
            if l == 1:
                phases.append(("kv",))
        if cfg.stop is not None:
            phases = phases[:cfg.stop]
        if cfg.phases is not None:
            phases = cfg.phases
        for pi, ph in enumerate(phases):
            if pi + 1 < len(phases):
                hook = (lambda t, psb=6, g=phase_gi(phases[pi + 1]): norm_H_tile(g, t, psb))
            else:
                hook = (lambda t, psb=6, st=store_tile: (final_tile(t, psb), st(t))) if full else None
            pre = pi > 0
            if ph[0] == "ffn":
                ffn_phase(ph[1], ph[2], ph[3], prenormed=pre, on_final=hook)
            elif ph[0] == "gdn":
                gdn_phase(ph[1], s, prenormed=pre, on_final=hook)
            elif ph[0] == "mla":
                mla_phase(ph[1], prenormed=pre, on_final=hook)
            else:
                shared_kv_phase(s, prenormed=pre, on_final=hook)
        if full and not phases:
            for t in range(NT512):
                final_tile(t)
        if not (full and phases):
            tok = P.dma("sp", "st_y", [lambda e, s=s, k=k: e.dma_start(out=y_fm.ap()[s, :, k, :], in_=Hh[:, k, :]) for k in range(KC)],
                        reads=[("H", k, t) for k in range(KC) for t in range(NT512)], writes=["yout"])
            out_toks.append(tok)
    P.wait_all("sp", [("dma:" + k, v) for k, v in P.dma_cnt.items() if k.startswith("sp_")])

    P.finalize(sems)

    @block.sync
    def _(e):
        P.emit("sp", e)

    @block.gpsimd
    def _(e):
        P.emit("pool", e)

    @block.tensor
    def _(e):
        P.emit("pe", e)

    @block.scalar
    def _(e):
        P.emit("act", e)

    @block.vector
    def _(e):
        P.emit("dve", e)

    es.close()
    return nc, dbg_list


def host_layout(inp, nseq_total):
    f = lambda a: np.ascontiguousarray(np.asarray(a, dtype=np.float32))
    out = {}
    x = f(inp["x"])
    out["x_fm"] = np.ascontiguousarray(x.transpose(0, 2, 1).reshape(x.shape[0], KC, 128, S).transpose(0, 2, 1, 3))
    out["pos"] = np.ascontiguousarray(np.asarray(inp["positions"], dtype=np.int32))
    g = np.zeros((128, 128), np.float32)
    vecs = [f(inp["ffn1_norm"])[l] for l in range(4)] + [f(inp["mix_norm"])[l] for l in range(4)] + \
           [f(inp["ffn2_norm"])[l] for l in range(4)] + [f(inp["kv_norm"])]
    for i, v in enumerate(vecs):
        g[:, i * 8:(i + 1) * 8] = v.reshape(8, 128).T
    g[:, 104:106] = f(inp["mla_kv_a_norm"]).reshape(2, 128).T
    for j in range(2):
        g[:, 106 + 3 * j:109 + 3 * j] = f(inp["mla_q_norm"])[j].reshape(3, 128).T
    g[:, 112:114] = f(inp["gdn_out_norm"]).T
    g[:, 120:128] = f(inp["final_norm"]).reshape(8, 128).T
    out["gains"] = g
    cw = f(inp["gdn_conv_w"])
    out["convw"] = np.ascontiguousarray(cw.reshape(2, 4, 24, 128).transpose(3, 0, 1, 2).reshape(128, 192))
    rows = np.zeros((2, 2, 128), np.float32)
    for i in range(2):
        rows[i, 0] = np.repeat(f(inp["gdn_a_log"])[i], 16)
        rows[i, 1] = np.repeat(f(inp["gdn_dt_bias"])[i], 16)
    out["rows"] = rows
    wffn = np.empty((2, 4, NJG, 128, 6144), np.float32)
    for fi, (gu, dn) in enumerate([("ffn1_w_gu", "ffn1_w_down"), ("ffn2_w_gu", "ffn2_w_down")]):
        wgu = f(inp[gu])
        wd = f(inp[dn])
        a = wgu.reshape(4, 8, 128, 2, NJG, 256)
        wffn[fi, :, :, :, :4096] = a.transpose(0, 4, 2, 1, 3, 5).reshape(4, NJG, 128, 4096)
        b = wd.reshape(4, NJG, 2, 128, 1024)
        wffn[fi, :, :, :, 4096:] = b.transpose(0, 1, 3, 2, 4).reshape(4, NJG, 128, 2048)
    out["wffn"] = wffn
    win = f(inp["gdn_w_in"])
    a = win[:, :, :4096].reshape(2, 8, 128, 4, 8, 128)
    out["win"] = np.ascontiguousarray(a.transpose(0, 4, 2, 1, 3, 5).reshape(2, 8, 128, 4096))
    out["wab"] = np.ascontiguousarray(win[:, :, 4096:].reshape(2, 8, 128, 16).transpose(0, 2, 1, 3).reshape(2, 128, 128))
    out["wout"] = np.ascontiguousarray(f(inp["gdn_w_out"]).reshape(2, 8, 128, 1024))
    perm = np.concatenate([np.arange(32, 64), np.arange(0, 32)])
    wkva = f(inp["mla_w_kv_a"])
    wk = np.concatenate([wkva, wkva[:, 256:320][:, perm]], axis=1)
    out["wkva"] = np.ascontiguousarray(wk.reshape(8, 128, 384).transpose(1, 0, 2).reshape(128, 8 * 384))
    out["wkvb"] = np.ascontiguousarray(f(inp["mla_w_kv_b"]).reshape(2, 128, 2048).transpose(1, 0, 2).reshape(128, 4096))
    out["wdq"] = np.ascontiguousarray(f(inp["mla_w_dq"]).reshape(2, 8, 128, 384).transpose(0, 2, 1, 3).reshape(2, 128, 3072))
    wuq = f(inp["mla_w_uq"]).reshape(2, 3, 128, 8, 192)
    wuqx = np.concatenate([wuq, wuq[..., 128:192][..., perm]], axis=-1)
    out["wuq"] = np.ascontiguousarray(wuqx.transpose(0, 2, 1, 3, 4).reshape(2, 128, 3 * 8 * 256))
    out["wo"] = np.ascontiguousarray(f(inp["mla_w_o"]).reshape(2, 8, 128, 1024).transpose(0, 2, 1, 3).reshape(2, 128, 8192))
    out["consts"] = CONSTS
    return out


def run(inputs, cfg, ncores=NCORES, batches=None):
    hl = host_layout(inputs, None)
    nc, dbg_list = build(cfg)
    B = hl["x_fm"].shape[0]
    if batches is None:
        batches = [list(range(c * cfg.nseq, (c + 1) * cfg.nseq)) for c in range(ncores)]
    in_maps = []
    shared = {k: v for k, v in hl.items() if k not in ("x_fm", "pos")}
    for bl in batches:
        m = dict(shared)
        m["x_fm"] = np.ascontiguousarray(hl["x_fm"][bl])
        m["pos"] = np.ascontiguousarray(hl["pos"][bl])
        in_maps.append(m)
    res = run_bass_kernel_spmd(nc, in_maps, core_ids=list(range(len(batches))))
    ys = []
    for r in res.results:
        y = r["y_fm"]
        ys.append(y.transpose(0, 3, 2, 1).reshape(y.shape[0], S, D))
    return np.concatenate(ys, axis=0), res, dbg_list


def kernel(**inputs):
    y, _, _ = run(inputs, Cfg(nseq=2))
    return np.ascontiguousarray(y.astype(np.float32))
```
